# Optimizing a Trainium2 kernel written in Bass

```python
import math
import jax, jax.numpy as jnp
from jax import lax
import numpy as np

D_MODEL = 1024
BATCH = 32
SEQ = 2048
DEPTH = 1

SSD_D_INNER = 1024
SSD_HEAD_DIM = 64
SSD_HEADS = SSD_D_INNER // SSD_HEAD_DIM
SSD_GROUPS = 2
SSD_D_STATE = 128
SSD_CONV = 4
SSD_CHUNK = 128
SSD_CONV_DIM = SSD_D_INNER + 2 * SSD_GROUPS * SSD_D_STATE
SSD_NORM_EPS = 1e-5
DA_HEADS = 8
DA_HEAD_DIM = 64
DA_WIDTH = DA_HEADS * 2 * DA_HEAD_DIM
ROPE_THETA = 500000.0
ROPE_DIM = DA_HEAD_DIM // 4
Q_BLOCK = 128
SUBLN_EPS = 1e-5
D_FF = 2816
FFN_CONV = 3
N_BRANCH = 2
NORM_EPS = 1e-6
IN_SPLITS = (
    SSD_D_INNER,
    SSD_D_INNER + SSD_CONV_DIM,
    SSD_D_INNER + SSD_CONV_DIM + SSD_HEADS,
    SSD_D_INNER + SSD_CONV_DIM + SSD_HEADS + DA_WIDTH,
    SSD_D_INNER + SSD_CONV_DIM + SSD_HEADS + 2 * DA_WIDTH,
    SSD_D_INNER + SSD_CONV_DIM + SSD_HEADS + 3 * DA_WIDTH,
)
IN_WIDTH = SSD_D_INNER + SSD_CONV_DIM + SSD_HEADS + 3 * DA_WIDTH + N_BRANCH * D_MODEL

kernel_name = 'hybrid_ssd_diffattn_gated_convffn'


def rms_norm(x, w, eps=NORM_EPS):
    xf = x.astype(jnp.float32)
    y = xf * lax.rsqrt(jnp.mean(xf * xf, axis=-1, keepdims=True) + eps)
    return (y * w.astype(jnp.float32)).astype(x.dtype)


def causal_dwconv(x, w, b):
    k, ch = w.shape
    y = lax.conv_general_dilated(x, w[:, None, :].astype(x.dtype), window_strides=(1,),
                                 padding=[(k - 1, 0)], dimension_numbers=('NWC', 'WIO', 'NWC'),
                                 feature_group_count=ch)
    return y + b.astype(x.dtype)


def rope_tables(seq):
    pos = jnp.arange(seq, dtype=jnp.float32)
    inv_freq = jnp.power(ROPE_THETA, -jnp.arange(0, ROPE_DIM, 2, dtype=jnp.float32) / ROPE_DIM)
    ang = pos[:, None] * inv_freq[None, :]
    return jnp.cos(ang), jnp.sin(ang)


def partial_rope(t, cos, sin):
    half = ROPE_DIM // 2
    r1 = t[..., :half].astype(jnp.float32)
    r2 = t[..., half:ROPE_DIM].astype(jnp.float32)
    cs, sn = cos[None, :, None, :], sin[None, :, None, :]
    rot = jnp.concatenate([r1 * cs - r2 * sn, r1 * sn + r2 * cs], axis=-1).astype(t.dtype)
    return jnp.concatenate([rot, t[..., ROPE_DIM:]], axis=-1)


def ssd_chunked(xs, dt, a, bm, cm):
    bsz, seq, nh, hp = xs.shape
    g, n = bm.shape[2], bm.shape[3]
    r = nh // g
    nc, l = seq // SSD_CHUNK, SSD_CHUNK
    xd = (xs.astype(jnp.float32) * dt[..., None]).reshape(bsz, nc, l, g, r, hp)
    la = (dt * a).reshape(bsz, nc, l, g, r)
    bc = bm.astype(jnp.float32).reshape(bsz, nc, l, g, n)
    cc = cm.astype(jnp.float32).reshape(bsz, nc, l, g, n)
    a_cum = jnp.cumsum(la, axis=2)
    seg = a_cum[:, :, :, None] - a_cum[:, :, None, :]
    causal = jnp.tril(jnp.ones((l, l), dtype=bool))[:, :, None, None]
    decay = jnp.exp(jnp.where(causal, seg, -jnp.inf))
    cb = jnp.einsum('bclgn,bcsgn->bclsg', cc, bc)
    y_diag = jnp.einsum('bclsgr,bcsgrp->bclgrp', cb[..., None] * decay, xd)
    decay_to_end = jnp.exp(a_cum[:, :, -1:] - a_cum)
    states = jnp.einsum('bclgn,bclgrp->bcgrpn', bc, xd * decay_to_end[..., None])
    chunk_decay = jnp.exp(a_cum[:, :, -1])

    def step(carry, inp):
        st, dec = inp
        return carry * dec[..., None, None] + st, carry

    init = jnp.zeros((bsz, g, r, hp, n), jnp.float32)
    _, prev = lax.scan(step, init, (jnp.swapaxes(states, 0, 1), jnp.swapaxes(chunk_decay, 0, 1)))
    prev = jnp.swapaxes(prev, 0, 1)
    y_off = jnp.einsum('bclgn,bcgrpn->bclgrp', cc, prev) * jnp.exp(a_cum)[..., None]
    return (y_diag + y_off).reshape(bsz, seq, nh, hp)


def ssd_branch(z, xbc, dt_raw, conv_w, conv_b, dt_bias, a_log, d_skip, norm_w):
    bsz, seq, _ = z.shape
    xbc = jax.nn.silu(causal_dwconv(xbc, conv_w, conv_b))
    xs, bm, cm = jnp.split(xbc, [SSD_D_INNER, SSD_D_INNER + SSD_GROUPS * SSD_D_STATE], axis=-1)
    xs = xs.reshape(bsz, seq, SSD_HEADS, SSD_HEAD_DIM)
    bm = bm.reshape(bsz, seq, SSD_GROUPS, SSD_D_STATE)
    cm = cm.reshape(bsz, seq, SSD_GROUPS, SSD_D_STATE)
    dt = jax.nn.softplus(dt_raw.astype(jnp.float32) + dt_bias.astype(jnp.float32))
    a = -jnp.exp(a_log.astype(jnp.float32))
    y = ssd_chunked(xs, dt, a, bm, cm)
    y = y + d_skip.astype(jnp.float32)[:, None] * xs.astype(jnp.float32)
    y = y.reshape(bsz, seq, SSD_D_INNER) * jax.nn.silu(z.astype(jnp.float32))
    yg = y.reshape(bsz, seq, SSD_GROUPS, SSD_D_INNER // SSD_GROUPS)
    yg = yg * lax.rsqrt(jnp.mean(yg * yg, axis=-1, keepdims=True) + SSD_NORM_EPS)
    return (yg.reshape(bsz, seq, SSD_D_INNER) * norm_w.astype(jnp.float32)).astype(z.dtype)


def diff_attention_branch(q, k, v, lq1, lk1, lq2, lk2, subln_w, cos, sin, lam_init):
    bsz, seq, _ = q.shape
    q = partial_rope(q.reshape(bsz, seq, 2 * DA_HEADS, DA_HEAD_DIM), cos, sin).transpose(0, 2, 1, 3)
    k = partial_rope(k.reshape(bsz, seq, 2 * DA_HEADS, DA_HEAD_DIM), cos, sin).transpose(0, 2, 1, 3)
    v = v.reshape(bsz, seq, DA_HEADS, 2 * DA_HEAD_DIM).transpose(0, 2, 1, 3)
    f32 = jnp.float32
    lam = (jnp.exp(jnp.sum(lq1.astype(f32) * lk1.astype(f32)))
           - jnp.exp(jnp.sum(lq2.astype(f32) * lk2.astype(f32))) + lam_init)
    scale = DA_HEAD_DIM ** -0.5
    outs = []
    for i in range(seq // Q_BLOCK):
        q0, q1 = i * Q_BLOCK, (i + 1) * Q_BLOCK
        s = jnp.einsum('bhqd,bhkd->bhqk', q[:, :, q0:q1], k[:, :, :q1]).astype(f32) * scale
        mask = jnp.arange(q1)[None, :] <= jnp.arange(q0, q1)[:, None]
        p = jax.nn.softmax(jnp.where(mask, s, -jnp.inf), axis=-1)
        p = p.reshape(bsz, DA_HEADS, 2, Q_BLOCK, q1)
        attn = (p[:, :, 0] - lam * p[:, :, 1]).astype(v.dtype)
        outs.append(jnp.einsum('bhqk,bhkd->bhqd', attn, v[:, :, :q1]))
    o = jnp.concatenate(outs, axis=2)
    o = rms_norm(o, subln_w, eps=SUBLN_EPS) * (1.0 - lam_init)
    return o.transpose(0, 2, 1, 3).reshape(bsz, seq, DA_WIDTH)


def setup_inputs(seed: int = 0) -> dict:
    key = jax.random.key(seed)
    ks = jax.random.split(key, 32)
    nrm = lambda k, shape, s: jax.random.normal(k, shape, jnp.float32) * s
    L = DEPTH
    dt0 = jnp.exp(jax.random.uniform(ks[7], (L, SSD_HEADS)) * (math.log(0.1) - math.log(0.001)) + math.log(0.001))
    return {
        'x': nrm(ks[0], (BATCH, SEQ, D_MODEL), 1.0),
        'c': nrm(ks[1], (BATCH, D_MODEL), 1.0),
        'w_ada': nrm(ks[2], (L, D_MODEL, 6 * D_MODEL), D_MODEL ** -0.5),
        'b_ada': nrm(ks[3], (L, 6 * D_MODEL), 0.02),
        'pre_norm1_w': 1.0 + nrm(ks[4], (L, D_MODEL), 0.05),
        'w_in': nrm(ks[5], (L, D_MODEL, IN_WIDTH), D_MODEL ** -0.5),
        'conv_ssd_w': nrm(ks[6], (L, SSD_CONV, SSD_CONV_DIM), SSD_CONV ** -0.5),
        'conv_ssd_b': nrm(ks[8], (L, SSD_CONV_DIM), 0.02),
        'dt_bias': dt0 + jnp.log(-jnp.expm1(-dt0)),
        'a_log': jnp.log(jax.random.uniform(ks[9], (L, SSD_HEADS), jnp.float32, 1.0, 16.0)),
        'd_skip': 1.0 + nrm(ks[10], (L, SSD_HEADS), 0.1),
        'ssd_norm_w': 1.0 + nrm(ks[11], (L, SSD_D_INNER), 0.05),
        'w_ssd_o': nrm(ks[12], (L, SSD_D_INNER, D_MODEL), SSD_D_INNER ** -0.5),
        'lambda_q1': nrm(ks[13], (L, DA_HEAD_DIM), 0.1),
        'lambda_k1': nrm(ks[14], (L, DA_HEAD_DIM), 0.1),
        'lambda_q2': nrm(ks[15], (L, DA_HEAD_DIM), 0.1),
        'lambda_k2': nrm(ks[16], (L, DA_HEAD_DIM), 0.1),
        'subln_w': 1.0 + nrm(ks[17], (L, 2 * DA_HEAD_DIM), 0.05),
        'w_attn_o': nrm(ks[18], (L, DA_WIDTH, D_MODEL), DA_WIDTH ** -0.5),
        'w_out': nrm(ks[19], (L, D_MODEL, D_MODEL), D_MODEL ** -0.5),
        'post_norm1_w': 1.0 + nrm(ks[20], (L, D_MODEL), 0.05),
        'pre_norm2_w': 1.0 + nrm(ks[21], (L, D_MODEL), 0.05),
        'w_up': nrm(ks[22], (L, D_MODEL, 2 * D_FF), D_MODEL ** -0.5),
        'conv_ffn_w': nrm(ks[23], (L, FFN_CONV, 2 * D_FF), FFN_CONV ** -0.5),
        'conv_ffn_b': nrm(ks[24], (L, 2 * D_FF), 0.02),
        'w_down': nrm(ks[25], (L, D_FF, D_MODEL), D_FF ** -0.5),
        'post_norm2_w': 1.0 + nrm(ks[26], (L, D_MODEL), 0.05),
    }


def reference(x, c, w_ada, b_ada, pre_norm1_w, w_in, conv_ssd_w, conv_ssd_b, dt_bias, a_log,
              d_skip, ssd_norm_w, w_ssd_o, lambda_q1, lambda_k1, lambda_q2, lambda_k2, subln_w,
              w_attn_o, w_out, post_norm1_w, pre_norm2_w, w_up, conv_ffn_w, conv_ffn_b, w_down,
              post_norm2_w):
    bsz, seq, _ = x.shape
    cos, sin = rope_tables(seq)
    for layer in range(DEPTH):
        lam_init = 0.8 - 0.6 * math.exp(-0.3 * layer)
        mod = jax.nn.silu(c) @ w_ada[layer] + b_ada[layer]
        sh1, sc1, g1, sh2, sc2, g2 = [m[:, None, :] for m in jnp.split(mod, 6, axis=-1)]
        h = rms_norm(x, pre_norm1_w[layer]) * (1.0 + sc1) + sh1
        proj = h @ w_in[layer]
        z, xbc, dt_raw, q, k, v, gates = jnp.split(proj, IN_SPLITS, axis=-1)
        y_ssd = ssd_branch(z, xbc, dt_raw, conv_ssd_w[layer], conv_ssd_b[layer], dt_bias[layer],
                           a_log[layer], d_skip[layer], ssd_norm_w[layer])
        y_att = diff_attention_branch(q, k, v, lambda_q1[layer], lambda_k1[layer], lambda_q2[layer],
                                      lambda_k2[layer], subln_w[layer], cos, sin, lam_init)
        gate_ssd, gate_att = jnp.split(jax.nn.sigmoid(gates), N_BRANCH, axis=-1)
        merged = gate_ssd * (y_ssd @ w_ssd_o[layer]) + gate_att * (y_att @ w_attn_o[layer])
        mix = merged @ w_out[layer]
        x = x + g1 * rms_norm(mix, post_norm1_w[layer])
        h2 = rms_norm(x, pre_norm2_w[layer]) * (1.0 + sc2) + sh2
        u = causal_dwconv(h2 @ w_up[layer], conv_ffn_w[layer], conv_ffn_b[layer])
        u_gate, u_val = jnp.split(u, 2, axis=-1)
        f = (jax.nn.gelu(u_gate, approximate=True) * u_val) @ w_down[layer]
        x = x + g2 * rms_norm(f, post_norm2_w[layer])
    return x
```

```python
import math
import numpy as np
import concourse.bass as bass
import concourse.mybir as mybir
from concourse.bass_utils import run_bass_kernel_spmd

F32 = mybir.dt.float32
BF16 = mybir.dt.bfloat16
I32 = mybir.dt.int32
ALU = mybir.AluOpType
AF = mybir.ActivationFunctionType
AX = mybir.AxisListType

NCORES = 8
D = 1024
SEQ = 2048
TB = 256
NT = TB // 128
DFF = 2816
NJ = DFF // 128
INW = 7696
C_Z, C_X, C_B, C_C, C_DT, C_Q, C_K, C_V, C_GS, C_GA = 0, 1024, 2048, 2304, 2560, 2576, 3600, 4624, 5648, 6672
ROPE_THETA = 500000.0
PI = math.pi


class View:
    __slots__ = ("buf", "ap")

    def __init__(self, buf, ap):
        self.buf = buf
        self.ap = ap


class Buf:
    def __init__(self, ctx, name, shape, dtype, space="sbuf"):
        self.ctx = ctx
        self.name = name
        self.space = space
        ctx.allbufs.append(self)
        nc = ctx.nc
        if space == "sbuf":
            self.t = nc.alloc_sbuf_tensor(name, list(shape), dtype)
        elif space == "psum":
            self.t = nc.alloc_psum_tensor(name, list(shape), dtype)
        else:
            self.t = nc.dram_tensor(name, list(shape), dtype)
        self.wr = {}
        self.rd = {}
        self.dkey = None
        self.dcnt = 0
        self.aliases = []

    def __getitem__(self, idx):
        return View(self, self.t[idx])

    def v(self, ap):
        return View(self, ap)


class VBuf(Buf):
    def __init__(self, ctx, name, ap, aliases=()):
        self.ctx = ctx
        self.name = name
        self.space = "sbuf"
        ctx.allbufs.append(self)
        self.t = ap
        self.wr = {}
        self.rd = {}
        self.dkey = None
        self.dcnt = 0
        self.aliases = list(aliases)


def _mx(d, k, v):
    if d.get(k, 0) < v:
        d[k] = v


class Ctx:
    ENG = ("pe", "act", "dve", "pool", "sp")

    def __init__(self, nc):
        self.nc = nc
        self.e = {"pe": nc.tensor, "act": nc.scalar, "dve": nc.vector,
                  "pool": nc.gpsimd, "sp": nc.sync}
        self.sems = {}
        self.ekey = {}
        self.ecnt = {}
        self.seen = {k: {} for k in self.ENG}
        for k in self.ENG:
            key = "e_" + k
            self.sems[key] = nc.alloc_semaphore(key)
            self.ekey[k] = key
            self.ecnt[k] = 0
        self.ninstr = 0
        self.nwaits = 0
        self.allbufs = []
        self.nop = {}
        self.marks = []
        self.dcnt = {}
        self._ps = []
        self._psi = 0

    def buf(self, name, shape, dtype, space="sbuf"):
        return Buf(self, name, shape, dtype, space)

    def _wait(self, eng, need):
        E = self.e[eng]
        seen = self.seen[eng]
        for k, v in need.items():
            if seen.get(k, 0) < v:
                E.wait_ge(self.sems[k], v)
                seen[k] = v
                self.nwaits += 1

    def op(self, eng, emit, reads=(), writes=(), signal=True):
        need = {}
        own = self.ekey[eng]
        rb = [r.buf if isinstance(r, View) else r for r in reads]
        wb = [w.buf if isinstance(w, View) else w for w in writes]
        for b in rb:
            for k, v in b.wr.items():
                _mx(need, k, v)
            if b.space == "psum":
                for k, v in b.rd.items():
                    if k != own:
                        _mx(need, k, v)
        for b0 in wb:
            for b in [b0] + b0.aliases:
                for k, v in b.wr.items():
                    if k != own or eng != "pe":
                        _mx(need, k, v)
                for k, v in b.rd.items():
                    if k != own or eng != "pe":
                        _mx(need, k, v)
        for b0 in rb:
            for b in b0.aliases:
                for k, v in b.wr.items():
                    _mx(need, k, v)
        self._wait(eng, need)
        ins = emit(self.e[eng])
        self.ninstr += 1
        self.nop[eng] = self.nop.get(eng, 0) + 1
        if signal:
            self.ecnt[eng] += 1
            ins.then_inc(self.sems[own], 1)
            t = self.ecnt[eng]
        else:
            t = self.ecnt[eng] + 1
        for b in rb:
            _mx(b.rd, own, t)
        for b in wb:
            _mx(b.wr, own, t)
        return ins

    def dma(self, eng, out, in_, **kw):
        need = {}
        tracked = []
        if isinstance(out, View):
            b = out.buf
            for d in (b.wr, b.rd):
                for k, v in d.items():
                    _mx(need, k, v)
            tracked.append((b, "w"))
            oap = out.ap
        else:
            oap = out
        if isinstance(in_, View):
            b = in_.buf
            for k, v in b.wr.items():
                _mx(need, k, v)
            tracked.append((b, "r"))
            iap = in_.ap
        else:
            iap = in_
        self._wait(eng, need)
        b0 = tracked[0][0]
        dkey = "d_%s_%s" % (eng, b0.name)
        if dkey not in self.sems:
            self.sems[dkey] = self.nc.alloc_semaphore(dkey)
            self.dcnt[dkey] = 0
        self.dcnt[dkey] += 16
        ins = self.e[eng].dma_start(out=oap, in_=iap, **kw)
        ins.then_inc(self.sems[dkey], 16)
        self.ninstr += 1
        for b, m in tracked:
            if m == "w":
                _mx(b.wr, dkey, self.dcnt[dkey])
            else:
                _mx(b.rd, dkey, self.dcnt[dkey])
        return ins

    def finish(self, eng, bufs):
        need = {}
        for b in bufs:
            for d in (b.wr, b.rd):
                for k, v in d.items():
                    _mx(need, k, v)
        self._wait(eng, need)

    def psum(self, hold=False):
        n = len(self._ps)
        for _ in range(n):
            b = self._ps[self._psi % n]
            self._psi += 1
            if not getattr(b, "held", False):
                break
        else:
            raise RuntimeError("all PSUM banks held")
        b.held = hold
        return b

    def release(self, b):
        b.held = False

    def mm(self, out, lhsT, rhs, start=True, stop=True, signal=None, serialize=False):
        if signal is None:
            signal = stop
        if serialize:
            own = self.ekey["pe"]
            v = out.buf.wr.get(own, 0)
            if v and self.seen["pe"].get(own, 0) < v:
                self.e["pe"].wait_ge(self.sems[own], v)
                self.seen["pe"][own] = v
        return self.op("pe", lambda E: E.matmul(out.ap, lhsT.ap, rhs.ap, start=start, stop=stop),
                       [lhsT, rhs], [out], signal=signal)

    def tr(self, out, in_, ident, signal=True):
        return self.op("pe", lambda E: E.transpose(out.ap, in_.ap, ident.ap), [in_, ident], [out],
                       signal=signal)

    def act(self, out, in_, func, scale=1.0, bias=None, accum=None, eng="act"):
        reads = [in_]
        writes = [out]
        kw = {}
        if isinstance(scale, View):
            reads.append(scale)
            kw["scale"] = scale.ap
        else:
            kw["scale"] = float(scale)
        if bias is not None:
            if isinstance(bias, View):
                reads.append(bias)
                kw["bias"] = bias.ap
            else:
                kw["bias"] = float(bias)
        if accum is not None:
            writes.append(accum)
            kw["accum_out"] = accum.ap
        return self.op(eng, lambda E: E.activation(out.ap, in_.ap, func, **kw), reads, writes)

    def tt(self, out, in0, in1, op, eng="dve"):
        return self.op(eng, lambda E: E.tensor_tensor(out.ap, in0.ap, in1.ap, op), [in0, in1], [out])

    def ts(self, out, in0, s1, s2=None, op0=ALU.mult, op1=None, eng="dve"):
        reads = [in0]
        a1 = s1
        a2 = s2
        if isinstance(s1, View):
            reads.append(s1)
            a1 = s1.ap
        if isinstance(s2, View):
            reads.append(s2)
            a2 = s2.ap
        if op1 is None:
            return self.op(eng, lambda E: E.tensor_scalar(out.ap, in0.ap, a1, a2, op0), reads, [out])
        return self.op(eng, lambda E: E.tensor_scalar(out.ap, in0.ap, a1, a2, op0, op1), reads, [out])

    def stt(self, out, in0, s, in1, op0, op1, eng="dve"):
        reads = [in0, in1]
        a = s
        if isinstance(s, View):
            reads.append(s)
            a = s.ap
        return self.op(eng, lambda E: E.scalar_tensor_tensor(out.ap, in0.ap, a, in1.ap, op0, op1),
                       reads, [out])

    def copy(self, out, in_, eng="dve"):
        return self.op(eng, lambda E: E.tensor_copy(out.ap, in_.ap), [in_], [out])

    def memset(self, out, val, eng="dve"):
        return self.op(eng, lambda E: E.memset(out.ap, val), [], [out])

    def recip(self, out, in_):
        return self.op("dve", lambda E: E.reciprocal(out.ap, in_.ap), [in_], [out])


def _param_layout():
    items = [("pn1", 8), ("pn2", 8), ("cws", 48), ("cbs", 12), ("cwf", 132), ("cbf", 44),
             ("subw", 1), ("bada", 48), ("invf", 1), ("dtb", 16), ("alog", 16), ("dsk", 16),
             ("lam", 256), ("snw", 1024)]
    off = {}
    o = 0
    for n, w in items:
        off[n] = (o, w)
        o += w
    return off, o


PO, NP = _param_layout()
CO = {"ident": (0, 128), "U": (128, 128), "tri": (256, 128), "sigT": (384, 128), "iota": (512, 256),
      "mask0": (768, 256), "mask1": (1024, 256)}
NCST = 1280
RO = {"bada": (0, 6144), "post1": (6144, 1024), "post2": (7168, 1024)}
NROW = 8192


def _col(v, n):
    return np.ascontiguousarray(np.asarray(v, np.float32).reshape(n, 128).T)


def _rep(v):
    v = np.asarray(v, np.float32).reshape(1, -1)
    return np.repeat(v, 128, axis=0)


def _host_params(inp):
    P = np.zeros((128, NP), np.float32)

    def put(name, arr):
        o, w = PO[name]
        P[:, o:o + w] = np.asarray(arr, np.float32).reshape(128, w)

    put("pn1", _col(inp["pre_norm1_w"][0], 8))
    put("pn2", _col(inp["pre_norm2_w"][0], 8))
    put("cws", np.asarray(inp["conv_ssd_w"][0]).reshape(4, 12, 128).transpose(2, 1, 0))
    put("cbs", _col(inp["conv_ssd_b"][0], 12))
    put("cwf", np.asarray(inp["conv_ffn_w"][0]).reshape(3, 44, 128).transpose(2, 1, 0))
    put("cbf", _col(inp["conv_ffn_b"][0], 44))
    put("subw", np.asarray(inp["subln_w"][0]).reshape(128, 1))
    put("bada", _col(inp["b_ada"][0], 48))
    invf = np.zeros((128, 1), np.float32)
    for p in range(128):
        d = p % 64
        if d < 16:
            invf[p, 0] = np.float32(np.power(np.float32(ROPE_THETA), np.float32(-(d % 8) / 8.0)))
    put("invf", invf)
    put("dtb", _rep(inp["dt_bias"][0]))
    put("alog", _rep(inp["a_log"][0]))
    put("dsk", _rep(inp["d_skip"][0]))
    put("lam", _rep(np.concatenate([inp["lambda_q1"][0], inp["lambda_k1"][0],
                                    inp["lambda_q2"][0], inp["lambda_k2"][0]])))
    put("snw", _rep(inp["ssd_norm_w"][0]))
    R = np.zeros((1, NROW), np.float32)
    R[0, 0:6144] = inp["b_ada"][0]
    R[0, 6144:7168] = inp["post_norm1_w"][0]
    R[0, 7168:8192] = inp["post_norm2_w"][0]
    return P, R


def _host_consts():
    C = np.zeros((128, NCST), np.float32)
    i = np.arange(128)
    C[:, 0:128] = np.eye(128, dtype=np.float32)
    C[:, 128:256] = (i[:, None] > i[None, :]).astype(np.float32)
    tri = (i[:, None] <= i[None, :]).astype(np.float32)
    C[:, 256:384] = tri
    sig = np.zeros((128, 128), np.float32)
    for cb in (0, 64):
        for d in range(8):
            sig[cb + d + 8, cb + d] = -1.0
            sig[cb + d, cb + d + 8] = 1.0
    C[:, 384:512] = sig
    C[:, 512:768] = np.arange(256, dtype=np.float32)[None, :]
    C[:, 768:896] = tri
    C[:, 896:1024] = 1.0
    C[:, 1024:1152] = 0.0
    C[:, 1152:1280] = tri
    return C


class _Stop(Exception):
    pass


STOP = [0]


def build_nc(NSEQ=4, NBLK=8):
    try:
        return _build_nc(NSEQ, NBLK)
    except _Stop as e:
        nc, c = e.args
        for eng in ("sp", "pool", "act", "dve", "pe"):
            c.finish(eng, c.allbufs)
        return nc, c


def _build_nc(NSEQ=4, NBLK=8):
    nc = bass.Bass("TRN2", target_bir_lowering=False)
    T = NBLK * TB
    x_d = nc.dram_tensor("x", [NSEQ, T, D], F32, kind="ExternalInput").ap()
    cT_d = nc.dram_tensor("cT", [128, 8, NSEQ], F32, kind="ExternalInput").ap()
    wada_d = nc.dram_tensor("w_ada", [D, 6 * D], F32, kind="ExternalInput").ap()
    win_d = nc.dram_tensor("w_in", [D, INW], F32, kind="ExternalInput").ap()
    wso_d = nc.dram_tensor("w_ssd_o", [D, D], F32, kind="ExternalInput").ap()
    wao_d = nc.dram_tensor("w_attn_o", [D, D], F32, kind="ExternalInput").ap()
    wout_d = nc.dram_tensor("w_out", [D, D], F32, kind="ExternalInput").ap()
    wup_d = nc.dram_tensor("w_up", [D, 2 * DFF], F32, kind="ExternalInput").ap()
    wdn_d = nc.dram_tensor("w_down", [DFF, D], F32, kind="ExternalInput").ap()
    par_d = nc.dram_tensor("params", [128, NP], F32, kind="ExternalInput").ap()
    row_d = nc.dram_tensor("rows", [1, NROW], F32, kind="ExternalInput").ap()
    cst_d = nc.dram_tensor("consts", [128, NCST], F32, kind="ExternalInput").ap()
    y_d = nc.dram_tensor("y", [NSEQ, T, D], F32, kind="ExternalOutput").ap()

    c = Ctx(nc)

    def ck(k):
        c.marks.append((k, dict(c.nop)))
        if STOP[0] == k:
            raise _Stop(nc, c)

    c._ps = [c.buf("ps%d" % i, [128, 512], F32, "psum") for i in range(7)]

    def psacc():
        return c.psum()

    tpb = [c.buf("tp%d" % i, [128, 8, 128], BF16, "psum") for i in range(1)]
    tpi = [0]

    def tpbank():
        b = tpb[tpi[0] % len(tpb)]
        tpi[0] += 1
        return b

    cst = c.buf("cst", [128, 512], F32)
    par = c.buf("par", [128, NP], F32)
    c.dma("sp", cst[:, 0:256], cst_d[:, 128:384])
    c.dma("sp", cst[:, 256:512], cst_d[:, 512:768])
    c.dma("sp", par[:], par_d)

    def P(name, a=None, b=None):
        o, w = PO[name]
        lo = o if a is None else o + a
        hi = o + w if b is None else o + b
        return par[:, lo:hi]

    def CS(name):
        o, w = {"U": (0, 128), "tri": (128, 128), "iota": (256, 256)}[name]
        return cst[:, o:o + w]

    identb = c.buf("identb", [128, 128], BF16)
    trib = c.buf("trib", [128, 128], BF16)
    sigTb = c.buf("sigTb", [128, 128], BF16)
    onesb = c.buf("onesb", [128, 128], BF16)
    onesf = c.buf("onesf", [128, 128], F32)
    maskb = c.buf("maskb", [128, 2, 256], BF16)
    c.dma("pool", identb[:], cst_d[:, 0:128])
    c.dma("pool", trib[:], cst_d[:, 256:384])
    c.dma("pool", sigTb[:], cst_d[:, 384:512])
    c.memset(onesb[:], 1.0)
    c.memset(onesf[:], 1.0)
    c.dma("pool", maskb[:, 0, :], cst_d[:, 768:1024])
    c.dma("pool", maskb[:, 1, :], cst_d[:, 1024:1280])
    Uf = CS("U")
    trif = CS("tri")
    iota = CS("iota")

    Wb = {}
    for name, ap, KC, N in (("in", win_d, 8, INW), ("so", wso_d, 8, D), ("ao", wao_d, 8, D),
                            ("out", wout_d, 8, D), ("up", wup_d, 8, 2 * DFF), ("dn", wdn_d, NJ, D)):
        Wb[name] = c.buf("wb_" + name, [128, KC, N], BF16, "dram")
        src = ap.rearrange("(kc p) n -> p kc n", p=128)
        for kc in range(KC):
            c.dma("pool", Wb[name][:, kc, :], src[:, kc, :])
    wada_src = wada_d.rearrange("(kc p) n -> p kc n", p=128)

    NSLAB = 4
    slabs = [c.buf("slab%d" % i, [128, 8, 512], BF16) for i in range(NSLAB)]
    sli = [0]

    def load_slab(name, k0, nk, col0, ncols):
        s = slabs[sli[0] % NSLAB]
        sli[0] += 1
        c.dma("sp", s[:, 0:nk, 0:ncols], Wb[name][:, k0:k0 + nk, col0:col0 + ncols])
        return s

    def load_slab_ada(col0, ncols):
        s = slabs[sli[0] % NSLAB]
        sli[0] += 1
        c.dma("pool", s[:, 0:8, 0:ncols], wada_src[:, :, col0:col0 + ncols])
        return s

    Wdt = c.buf("Wdt", [128, 8, 16], BF16)
    c.dma("sp", Wdt[:], Wb["in"][:, :, C_DT:C_DT + 16])

    sm = c.buf("sm", [128, 64], F32)
    lamt = c.buf("lamt", [128, 128], F32)
    c.tt(lamt[:, 0:64], P("lam", 0, 64), P("lam", 64, 128), ALU.mult)
    c.tt(lamt[:, 64:128], P("lam", 128, 192), P("lam", 192, 256), ALU.mult)
    c.op("dve", lambda E: E.reduce_sum(sm[:, 0:1].ap, lamt[:, 0:64].ap, axis=AX.X), [lamt], [sm])
    c.op("dve", lambda E: E.reduce_sum(sm[:, 1:2].ap, lamt[:, 64:128].ap, axis=AX.X), [lamt], [sm])
    c.act(sm[:, 2:4], sm[:, 0:2], AF.Exp)
    c.tt(sm[:, 4:5], sm[:, 3:4], sm[:, 2:3], ALU.subtract)
    c.ts(sm[:, 5:6], sm[:, 4:5], -0.2, None, ALU.add)
    nlam = sm[:, 5:6]
    c.ts(sm[:, 6:7], P("subw"), 0.8, None, ALU.mult)
    subw = sm[:, 6:7]
    arow = c.buf("arow", [128, 16], F32)
    c.act(arow[:], P("alog"), AF.Exp)
    c.ts(arow[:], arow[:], -1.0, None, ALU.mult)

    cTs = c.buf("cTs", [128, 8, NSEQ], F32)
    c.dma("sp", cTs[:], cT_d)
    scT = c.buf("scT", [128, 8, NSEQ], F32)
    c.act(scT[:], cTs[:], AF.Silu)
    scTb = c.buf("scTb", [128, 8, NSEQ], BF16)
    c.copy(scTb[:], scT[:])
    modT = c.buf("modT", [128, 32, NSEQ], F32)
    mod_cols = [0, 1024, 3072, 4096]
    psm = c.psum()
    for gi, col0 in enumerate(mod_cols):
        for half in range(2):
            s = load_slab_ada(col0 + half * 512, 512)
            for cc in range(4):
                ci = gi * 8 + half * 4 + cc
                for kc in range(8):
                    c.mm(psm[:, ci * NSEQ:(ci + 1) * NSEQ], s[:, kc, cc * 128:(cc + 1) * 128],
                         scTb[:, kc, :], start=(kc == 0), stop=(kc == 7))
    bo = PO["bada"][0]
    for gi, col0 in enumerate(mod_cols):
        fc = col0 // 128
        c.tt(modT[:, gi * 8:(gi + 1) * 8, :],
             psm.v(psm.t[:, gi * 8 * NSEQ:(gi + 1) * 8 * NSEQ].rearrange("p (a b) -> p a b", b=NSEQ)),
             par.v(par.t[:, bo + fc:bo + fc + 8].unsqueeze(2).to_broadcast([128, 8, NSEQ])), ALU.add)
    A1 = c.buf("A1", [128, 8, NSEQ], F32)
    A2 = c.buf("A2", [128, 8, NSEQ], F32)
    for Ab, pn, sc0 in ((A1, "pn1", 8), (A2, "pn2", 24)):
        o, w = PO[pn]
        c.ts(Ab[:], modT[:, sc0:sc0 + 8, :], 1.0, None, ALU.add)
        c.tt(Ab[:], Ab[:], par.v(par.t[:, o:o + 8].unsqueeze(2).to_broadcast([128, 8, NSEQ])), ALU.mult)

    ck(1)
    KT = c.buf("KT", [128, 8, T], BF16)
    Vt = c.buf("Vt", [128, T // 128, D], BF16)
    xb = [c.buf("xres0", [128, NT, D], F32)]
    HX = 3
    hT = c.buf("hT", [128, 8, HX + TB], BF16)
    h1halo = c.buf("h1halo", [128, 8, 3], BF16)
    h2halo = c.buf("h2halo", [128, 8, 2], BF16)
    qT = c.buf("qT", [128, 8, TB], BF16)
    yaT = c.buf("yaT", [128, 8, TB], BF16)
    ysT = c.buf("ysT", [128, 8, TB], BF16)
    mgT = qT
    xact = c.buf("xact", [128, 12, TB], BF16)
    zs = c.buf("zs", [128, NT, D], BF16)
    G1W = c.buf("G1W", [128, D], F32)
    G2W = c.buf("G2W", [128, D], F32)
    Sst = c.buf("Sst", [128, D], F32)
    Sbf = c.buf("Sbf", [128, D], BF16)
    cosb = c.buf("cosb", [128, TB], F32)
    sinb = c.buf("sinb", [128, TB], F32)
    scr = c.buf("scr", [128, 4096], F32)
    scr_bf = scr.t[:, :].bitcast(BF16)
    Rr = VBuf(c, "Rr", scr.t[:, 0:2048].rearrange("p (h l) -> p h l", h=16))
    Ee = VBuf(c, "Ee", scr_bf[:, 4096:6144].rearrange("p (h l) -> p h l", h=16))
    Mm = Ee
    ybuf = VBuf(c, "ybuf", scr.t[:, 3072:4096])
    mT = VBuf(c, "mT", scr_bf[:, 0:NJ * TB].rearrange("p (j t) -> p j t", j=NJ), aliases=[Rr, Ee])
    m1all = c.buf("m1all", [128, 4 * TB], F32)
    m1b = [VBuf(c, "m1b%d" % i, m1all.t[:, i * TB:(i + 1) * TB]) for i in range(4)]
    PaccV = [VBuf(c, "pacc%d" % i, m1all.t[:, 2 * i * TB:(2 * i + 2) * TB],
                  aliases=[m1b[2 * i], m1b[2 * i + 1]]) for i in range(2)]
    for i in range(4):
        m1b[i].aliases = [PaccV[i // 2]]
    Pb = c.buf("Pb", [128, 2 * TB], BF16)
    Rr.aliases = [mT]
    Ee.aliases = [mT]
    ytmp = c.buf("ytmp", [128, D], F32)
    junk = VBuf(c, "junk", ytmp.t[:, 0:512].bitcast(BF16), aliases=[ytmp])
    ytmp.aliases = [junk]
    xn = c.buf("xn", [128, D], BF16)
    ytok = xn
    ssq = c.buf("ssq", [128, 16], F32)
    qraw = [c.buf("qraw%d" % i, [128, TB], BF16) for i in range(2)]
    ptile = [c.buf("pt%d" % i, [128, 2 * TB], BF16) for i in range(4)]
    tfp = [c.buf("tf%d" % i, [128, TB + 4], F32) for i in range(8)]
    tfi = [0]

    def TF():
        bb = tfp[tfi[0] % len(tfp)]
        tfi[0] += 1
        return bb

    dtv = c.buf("dtv", [128, NT, 16], F32)
    lav = c.buf("lav", [128, NT, 16], F32)
    dtt = c.buf("dtt", [128, 16], F32)
    xs_tok = c.buf("xs_tok", [128, D], BF16)
    B_tok = c.buf("B_tok", [128, 2, 128], BF16)
    xd = c.buf("xd", [128, 16, 64], BF16)
    xdd = c.buf("xdd", [128, 16, 64], BF16)
    Gm = c.buf("Gm", [128, 2, 128], BF16)
    sml = c.buf("sml", [128, 80], F32)
    onesrow = onesf[0:1, :]

    def rstd_from_ss(dst, ss, n, eps):
        c.act(dst, ss, AF.Ln, scale=1.0 / n, bias=eps)
        c.act(dst, dst, AF.Exp, scale=-0.5)

    def norm_to_T(xsrc, Ab, boff, b, hbuf, H):
        for i in range(NT):
            c.act(junk[:], xsrc[:, i, :], AF.Square, accum=ssq[:, i:i + 1])
            rstd_from_ss(ssq[:, 4 + i:5 + i], ssq[:, i:i + 1], D, 1e-6)
            c.ts(xn[:], xsrc[:, i, :], ssq[:, 4 + i:5 + i], None, ALU.mult)
            tp = tpbank()
            for kc in range(8):
                c.tr(tp[:, kc, :], xn[:, kc * 128:(kc + 1) * 128], identb[:], signal=(kc == 7))
            for kc in range(8):
                c.act(hT[:, kc, HX + i * 128:HX + (i + 1) * 128], tp[:, kc, :], AF.Identity,
                      scale=Ab[:, kc, b:b + 1], bias=modT[:, boff + kc, b:b + 1])
        c.copy(hT[:, :, HX - H:HX], hbuf[:], eng="pool")
        c.copy(hbuf[:], hT[:, :, HX + TB - H:HX + TB], eng="pool")

    def pipeline(n, stages):
        ns = len(stages)
        st = [dict() for _ in range(n)]
        for t in range(n + ns - 1):
            for si in range(ns):
                i = t - si
                if 0 <= i < n:
                    stages[si](i, st[i])

    def proj_fm(wname, col0, nchunks, rhsT, stages, pc=0):
        slab = {}

        def s0(ci, stt_):
            if ci % 4 == 0:
                n = min(4, nchunks - ci)
                slab["s"] = load_slab(wname, 0, 8, col0 + ci * 128, n * 128)
            s = slab["s"]
            ps = c.psum()
            w = slice((ci % 4) * 128, (ci % 4 + 1) * 128)
            for kc in range(8):
                c.mm(ps[:, 0:TB + pc], s[:, kc, w], rhsT[:, kc, HX - pc:HX + TB], start=(kc == 0), stop=(kc == 7))
            stt_["ps"] = ps

        pipeline(nchunks, [s0] + list(stages))

    def proj_tm(wname, col0, nslab, lhsT_buf, consume):
        slab = {}

        def s0(it, stt_):
            hf, i = divmod(it, NT)
            if i == 0:
                slab["s"] = load_slab(wname, 0, 8, col0 + hf * 512, 512)
            s = slab["s"]
            ps = c.psum()
            for kc in range(8):
                c.mm(ps[:, :], lhsT_buf[:, kc, HX + i * 128:HX + (i + 1) * 128], s[:, kc, :],
                     start=(kc == 0), stop=(kc == 7))
            stt_["ps"] = ps

        def s1(it, stt_):
            hf, i = divmod(it, NT)
            consume(hf, i, stt_["ps"])

        pipeline(nslab * NT, [s0, s1])

    def range_reduce_sin(dst, shift, ang, angk, angf, angm):
        src = ang
        if shift != 0.0:
            sB = TF()
            src = sB.v(sB.t[:, 0:TB])
            c.ts(src, ang, shift, None, ALU.add)
        c.ts(angk, src, 1.0 / (2 * PI), None, ALU.mult)
        c.copy(angm, angk)
        c.stt(angm, angm, -2 * PI, src, ALU.mult, ALU.add)
        c.ts(angf, angm, PI, 2 * PI, ALU.is_gt, ALU.mult)
        c.tt(angm, angm, angf, ALU.subtract)
        c.ts(angf, angm, -PI, 2 * PI, ALU.is_lt, ALU.mult)
        c.tt(angm, angm, angf, ALU.add)
        c.act(dst[:], angm, AF.Sin)

    for b in range(NSEQ):
        c.memset(Sst[:], 0.0)
        c.memset(Sbf[:], 0.0)
        c.memset(h1halo[:], 0.0)
        c.memset(h2halo[:], 0.0)
        screp = xn.v(xn.t[:, :].rearrange("p (a b) -> p a b", a=8))
        c.copy(screp, scT.v(scT.t[:, :, b:b + 1].to_broadcast([128, 8, 128])))
        for Gw, gcol, prow in ((G1W, 2048, "post1"), (G2W, 5120, "post2")):
            for half in range(2):
                s = load_slab_ada(gcol + half * 512, 512)
                rb = ytmp[0:1, 0:512]
                ro = RO["bada"][0] + gcol + half * 512
                c.dma("sp", rb, row_d[:, ro:ro + 512])
                psA = c.psum()
                for kc in range(8):
                    c.mm(psA[:, :], xn.v(xn.t[:, kc * 128:(kc + 1) * 128]), s[:, kc, :], start=(kc == 0), stop=False)
                c.mm(psA[:, :], onesrow, rb, start=False, stop=True)
                rb2 = ytmp[0:1, 512:1024]
                ro2 = RO[prow][0] + half * 512
                c.dma("sp", rb2, row_d[:, ro2:ro2 + 512])
                psB = c.psum()
                c.mm(psB[:, :], onesrow, rb2, start=True, stop=True)
                gtmp = ybuf[:, 0:512]
                c.act(gtmp, psB[:, :], AF.Identity)
                c.tt(Gw[:, half * 512:(half + 1) * 512], psA[:, :], gtmp, ALU.mult)

        ck(2)
        for blk in range(NBLK):
            t0 = blk * TB
            xres = xb[0]
            c.dma("pool", xres[:], x_d[b, t0:t0 + TB, :].rearrange("(i p) f -> p i f", p=128))
            angB, angkB, angfB, angmB = TF(), TF(), TF(), TF()
            ang = angB.v(angB.t[:, 0:TB])
            angk = angkB.v(angkB.t[:, 0:TB].bitcast(I32))
            angf = angfB.v(angfB.t[:, 0:TB])
            angm = angmB.v(angmB.t[:, 0:TB])
            c.ts(ang, iota, float(t0), None, ALU.add)
            c.ts(ang, ang, P("invf"), None, ALU.mult)
            range_reduce_sin(sinb, 0.0, ang, angk, angf, angm)
            range_reduce_sin(cosb, PI / 2, ang, angk, angf, angm)
            norm_to_T(xres, A1, 0, b, h1halo, 3)
            ck(3)

            def rope_stages(dst_fn):
                def r1(ci, stt_):
                    ps = stt_["ps"]
                    qr = qraw[ci % 2]
                    c.act(qr[:], ps[:, 0:TB], AF.Identity)
                    ps2 = c.psum()
                    c.mm(ps2[:, 0:TB], sigTb[:], qr[:])
                    t1 = TF()
                    c.tt(t1[:, 0:TB], ps[:, 0:TB], cosb[:], ALU.mult)
                    stt_["ps2"] = ps2
                    stt_["t1"] = t1

                def r2(ci, stt_):
                    t2 = TF()
                    c.tt(t2[:, 0:TB], stt_["ps2"][:, 0:TB], sinb[:], ALU.mult)
                    c.tt(dst_fn(ci), stt_["t1"][:, 0:TB], t2[:, 0:TB], ALU.add, eng="pool")
                return [r1, r2]

            proj_fm("in", C_K, 8, hT, rope_stages(lambda ci: KT[:, ci, t0:t0 + TB]))
            proj_fm("in", C_Q, 8, hT, rope_stages(lambda ci: qT[:, ci, :]))
            proj_tm("in", C_V, 2, hT, lambda hf, i, ps: c.act(
                Vt[:, blk * NT + i, hf * 512:(hf + 1) * 512], ps[:, :], AF.Identity))

            ck(4)
            def conv_stages(K, halo, wname, bname, chan_fn, fin):
                H = K - 1

                def c1(ci, stt_):
                    ps = stt_["ps"]
                    ch = chan_fn(ci)
                    acc = TF()[:, 0:TB]
                    wo = PO[wname][0] + ch * K
                    bo2 = PO[bname][0] + ch
                    c.act(acc, ps[:, H:H + TB], AF.Identity, scale=par[:, wo + H:wo + H + 1],
                          bias=par[:, bo2:bo2 + 1])
                    for k in range(H):
                        c.stt(acc, ps[:, k:k + TB], par[:, wo + k:wo + k + 1], acc, ALU.mult, ALU.add)
                    stt_["acc"] = acc

                def c2(ci, stt_):
                    fin(ci, stt_)
                return [c1, c2]

            for i in range(NT):
                ps = c.psum()
                for kc in range(8):
                    c.mm(ps[:, 0:16], hT[:, kc, HX + i * 128:HX + (i + 1) * 128], Wdt[:, kc, :],
                         start=(kc == 0), stop=(kc == 7))
                c.tt(dtt[:], ps[:, 0:16], P("dtb"), ALU.add)
                c.act(dtt[:], dtt[:], AF.Exp)
                c.act(dtv[:, i, :], dtt[:], AF.Ln, bias=1.0)
                c.tt(lav[:, i, :], dtv[:, i, :], arow[:], ALU.mult)
            proj_fm("in", C_X, 12, hT, conv_stages(
                4, None, "cws", "cbs", lambda ci: ci,
                lambda ci, stt_: c.act(xact[:, ci, :], stt_["acc"], AF.Silu)), pc=3)
            proj_tm("in", C_Z, 2, hT, lambda hf, i, ps: c.act(
                zs[:, i, hf * 512:(hf + 1) * 512], ps[:, :], AF.Silu))

            def ssd_gen():
                for i in range(NT):
                    cols = slice(i * 128, (i + 1) * 128)
                    tp = tpbank()
                    for c8 in range(8):
                        c.tr(tp[:, c8, :], xact[:, c8, cols], identb[:], signal=(c8 == 7))
                    c.copy(xs_tok.v(xs_tok.t[:, :].rearrange("p (a b) -> p a b", a=8)), tp[:, :, :])
                    tp2 = tpbank()
                    for g in range(2):
                        c.tr(tp2[:, g, :], xact[:, 8 + g, cols], identb[:], signal=(g == 1))
                    c.copy(B_tok[:], tp2[:, 0:2, :])
                    xs3 = xs_tok.v(xs_tok.t[:, :].rearrange("p (h d) -> p h d", h=16))
                    c.tt(xd[:], xs3, dtv.v(dtv.t[:, i, :].unsqueeze(2).to_broadcast([128, 16, 64])), ALU.mult)
                    yield
                    psg = c.psum()
                    for g in range(2):
                        c.mm(psg[:, g * 128:(g + 1) * 128], xact[:, 8 + g, cols], xact[:, 10 + g, cols])
                    c.tt(Gm[:], psg.v(psg.t[:, 0:256].rearrange("p (g l) -> p g l", g=2)),
                         cst.v(trif.ap.unsqueeze(1).to_broadcast([128, 2, 128])), ALU.mult)
                    c.tt(Rr[:], lav.v(lav.t[:, i, :].unsqueeze(2).to_broadcast([128, 16, 128])),
                         cst.v(trif.ap.unsqueeze(1).to_broadcast([128, 16, 128])), ALU.mult)
                    yield
                    for q4 in range(4):
                        pse = c.psum()
                        c.mm(pse[:, :], Uf, Rr.v(Rr.t[:, q4 * 4:(q4 + 1) * 4, :].rearrange("p h l -> p (h l)")))
                        c.act(Ee.v(Ee.t[:, q4 * 4:(q4 + 1) * 4, :].rearrange("p h l -> p (h l)")), pse[:, :], AF.Exp)
                        g = q4 // 2
                        c.tt(Mm[:, q4 * 4:(q4 + 1) * 4, :], Ee[:, q4 * 4:(q4 + 1) * 4, :],
                             Gm.v(Gm.t[:, g:g + 1, :].to_broadcast([128, 4, 128])), ALU.mult)
                        yield
                    psc = c.psum()
                    c.mm(psc[:, 0:16], trif, lav[:, i, :], signal=False)
                    c.mm(psc[:, 16:32], onesf[:], lav[:, i, :])
                    c.copy(sml[:, 0:16], psc[:, 0:16])
                    c.tt(sml[:, 16:32], psc[:, 16:32], sml[:, 0:16], ALU.subtract)
                    c.act(sml[:, 32:48], sml[:, 16:32], AF.Exp)
                    c.act(sml[:, 48:64], sml[:, 0:16], AF.Exp)
                    c.act(sml[:, 64:80], psc[:, 16:32], AF.Exp)
                    c.tt(xdd[:], xd[:], sml.v(sml.t[:, 32:48].unsqueeze(2).to_broadcast([128, 16, 64])), ALU.mult)
                    yield
                    psY = [psacc(), psacc()]
                    for h in range(16):
                        c.mm(psY[h // 8][:, (h % 8) * 64:(h % 8 + 1) * 64], Mm[:, h, :], xd[:, h, :],
                             signal=(h % 8 == 7))
                    psOf = [psacc(), psacc()]
                    for g in range(2):
                        c.mm(psOf[g][:, :], xact[:, 10 + g, cols], Sbf[:, g * 512:(g + 1) * 512])
                    for g in range(2):
                        hs = slice(g * 512, (g + 1) * 512)
                        c.tt(ybuf.v(ybuf.t[:, hs].rearrange("p (h d) -> p h d", h=8)),
                             psOf[g].v(psOf[g].t[:, :].rearrange("p (h d) -> p h d", h=8)),
                             sml.v(sml.t[:, 48 + g * 8:56 + g * 8].unsqueeze(2).to_broadcast([128, 8, 64])),
                             ALU.mult)
                        c.tt(ybuf[:, hs], ybuf[:, hs], psY[g][:, :], ALU.add)
                    yield
                    psS = [c.psum(), c.psum()]
                    for g in range(2):
                        c.mm(psS[g][:, :], B_tok[:, g, :],
                             xdd.v(xdd.t[:, g * 8:(g + 1) * 8, :].rearrange("p h d -> p (h d)")))
                    for g in range(2):
                        hs = slice(g * 512, (g + 1) * 512)
                        c.tt(Sst.v(Sst.t[:, hs].rearrange("p (h d) -> p h d", h=8)),
                             Sst.v(Sst.t[:, hs].rearrange("p (h d) -> p h d", h=8)),
                             sml.v(sml.t[:, 64 + g * 8:72 + g * 8].unsqueeze(2).to_broadcast([128, 8, 64])),
                             ALU.mult)
                        c.tt(Sst[:, hs], Sst[:, hs], psS[g][:, :], ALU.add)
                    c.copy(Sbf[:], Sst[:], eng="pool")
                    yield
                    c.tt(ytmp.v(ytmp.t[:, :].rearrange("p (h d) -> p h d", h=16)), xs3,
                         par.v(P("dsk").ap.unsqueeze(2).to_broadcast([128, 16, 64])), ALU.mult)
                    c.tt(ybuf[:], ybuf[:], ytmp[:], ALU.add)
                    c.tt(ybuf[:], ybuf[:], zs[:, i, :], ALU.mult)
                    yield
                    for g in range(2):
                        hs = slice(g * 512, (g + 1) * 512)
                        c.act(junk[:, 0:512], ybuf[:, hs], AF.Square, accum=ssq[:, 8 + g:9 + g])
                        rstd_from_ss(ssq[:, 10 + g:11 + g], ssq[:, 8 + g:9 + g], 512, 1e-5)
                        so = PO["snw"][0]
                        c.stt(ytok[:, hs], ybuf[:, hs], ssq[:, 10 + g:11 + g],
                              par[:, so + g * 512:so + (g + 1) * 512], ALU.mult, ALU.mult)
                    tp = tpbank()
                    for c8 in range(8):
                        c.tr(tp[:, c8, :], ytok[:, c8 * 128:(c8 + 1) * 128], identb[:], signal=(c8 == 7))
                    c.copy(ysT[:, :, cols], tp[:, :, :])
                    yield


            ck(5)
            nkb = NT * blk + NT
            LAG = 3 if nkb >= 6 else 2

            def attn_finalize(h, psO, Pacc):
                c.copy(Pb[:], Pacc[:, :], eng="pool")
                psL = c.psum()
                c.mm(psL[:, :], onesb[:], Pb[:])
                rr, on = TF(), TF()
                r2 = ytmp[:, 0:2 * TB]
                o2 = ytmp[:, 2 * TB:4 * TB]
                c.act(r2, psL[:, :], AF.Ln)
                c.act(r2, r2, AF.Exp, scale=-1.0)
                c.tt(o2, psO[:, :], r2, ALU.mult)
                c.release(psO)
                oo = on[:, 0:TB]
                sq = rr[:, 0:TB]
                sqb = qraw[h % 2]
                c.stt(oo, ytmp[:, 3 * TB:4 * TB], nlam, ytmp[:, 2 * TB:3 * TB], ALU.mult, ALU.add)
                c.act(sqb[:], oo, AF.Square)
                pss = c.psum()
                c.mm(pss[:, 0:TB], onesb[:], sqb[:])
                c.act(sq, pss[:, 0:TB], AF.Ln, scale=1.0 / 128, bias=1e-5)
                c.act(sq, sq, AF.Exp, scale=-0.5)
                c.stt(yaT[:, h, :], oo, subw, sq, ALU.mult, ALU.mult)

            sgen = ssd_gen()
            n_iter = 8 * (nkb + LAG)
            stride = max(1, n_iter // 26)
            it_cnt = [0]

            def ssd_step():
                it_cnt[0] += 1
                if it_cnt[0] % stride == 0:
                    next(sgen, None)

            pend = None
            for h in range(8):
                psO = c.psum(hold=True)
                Pacc = PaccV[h % 2]
                pts = {}
                for kb in range(nkb + LAG):
                    if kb < nkb:
                        pss = c.psum()
                        c.mm(pss[:, 0:TB], KT[0:64, h, kb * 128:(kb + 1) * 128], qT[0:64, h, :],
                             signal=(kb < LAG))
                    if kb >= LAG:
                        k2 = kb - LAG
                        c.mm(psO[:, :], Vt[:, k2, h * 128:(h + 1) * 128], pts[k2][:],
                             start=(k2 == 0), stop=(k2 == nkb - 1))
                    if kb < nkb:
                        c.mm(pss[:, TB:2 * TB], KT[64:128, h, kb * 128:(kb + 1) * 128], qT[64:128, h, :],
                             serialize=(kb < LAG))
                        pt = ptile[kb % 4]
                        c.act(pt[:], pss[:, :], AF.Exp, scale=0.125)
                        dk = kb - (nkb - NT)
                        if dk >= 0:
                            c.tt(pt.v(pt.t[:, :].rearrange("p (a b) -> p a b", a=2)),
                                 pt.v(pt.t[:, :].rearrange("p (a b) -> p a b", a=2)),
                                 maskb.v(maskb.t[:, dk:dk + 1, :].to_broadcast([128, 2, TB])), ALU.mult)
                        if kb == 0:
                            c.copy(Pacc[:, :], pt[:])
                        else:
                            c.tt(Pacc[:, :], Pacc[:, :], pt[:], ALU.add)
                        pts[kb] = pt
                    ssd_step()
                    if kb == LAG and pend is not None:
                        attn_finalize(*pend)
                        pend = None
                pend = (h, psO, Pacc)
            attn_finalize(*pend)
            for _ in sgen:
                pass

            ck(6)
            ck(7)
            for oc in range(0, 8, 4):
                for (wA, wG, cG, yT, first) in (("so", "in", C_GS, ysT, True), ("ao", "in", C_GA, yaT, False)):
                    s_a = load_slab(wA, 0, 8, oc * 128, 512)
                    s_g = load_slab(wG, 0, 8, cG + oc * 128, 512)

                    def g0(j, stt_, s_a=s_a, s_g=s_g, yT=yT):
                        w = slice(j * 128, (j + 1) * 128)
                        psA = c.psum()
                        for kc in range(8):
                            c.mm(psA[:, 0:TB], s_a[:, kc, w], yT[:, kc, :], start=(kc == 0), stop=(kc == 7))
                        psC = c.psum()
                        for kc in range(8):
                            c.mm(psC[:, 0:TB], s_g[:, kc, w], hT[:, kc, HX:HX + TB], start=(kc == 0), stop=(kc == 7))
                        stt_["A"] = psA
                        stt_["C"] = psC

                    def g1(j, stt_, first=first, oc=oc):
                        sg0 = TF()[:, 0:TB]
                        c.act(sg0, stt_["C"][:, 0:TB], AF.Sigmoid)
                        if first:
                            c.tt(m1b[j][:], stt_["A"][:, 0:TB], sg0, ALU.mult)
                        else:
                            mg2 = TF()[:, 0:TB]
                            c.tt(mg2, stt_["A"][:, 0:TB], sg0, ALU.mult)
                            c.tt(mgT[:, oc + j, :], m1b[j][:], mg2, ALU.add, eng="pool")

                    pipeline(4, [g0, g1])

            ck(8)
            def tm_norm_residual(wname, lhs_fn, nk, Gw):
                psF = [[psacc() for i in range(NT)] for hf in range(2)]
                for hf in range(2):
                    for k0 in range(0, nk, 8):
                        n = min(8, nk - k0)
                        s = load_slab(wname, k0, n, hf * 512, 512)
                        for i in range(NT):
                            for kk in range(n):
                                c.mm(psF[hf][i][:, :], lhs_fn(k0 + kk, i), s[:, kk, :],
                                     start=(k0 + kk == 0), stop=(k0 + kk == nk - 1))
                for i in range(NT):
                    c.act(junk[:, 0:512], psF[0][i][:, :], AF.Square, accum=ssq[:, 12:13])
                    c.act(junk[:, 512:1024], psF[1][i][:, :], AF.Square, accum=ssq[:, 13:14])
                    c.tt(ssq[:, 14:15], ssq[:, 12:13], ssq[:, 13:14], ALU.add)
                    rstd_from_ss(ssq[:, 15:16], ssq[:, 14:15], D, 1e-6)
                    for hf in range(2):
                        hs = slice(hf * 512, (hf + 1) * 512)
                        c.stt(ytmp[:, hs], psF[hf][i][:, :], ssq[:, 15:16], Gw[:, hs], ALU.mult, ALU.mult)
                    c.tt(xres[:, i, :], xres[:, i, :], ytmp[:], ALU.add)

            tm_norm_residual("out", lambda k, i: mgT[:, k, i * 128:(i + 1) * 128], 8, G1W)

            ck(9)
            norm_to_T(xres, A2, 16, b, h2halo, 2)
            fslab = {}

            def f0(jj, stt_):
                if jj % 4 == 0:
                    n = min(4, NJ - jj)
                    fslab["g"] = load_slab("up", 0, 8, jj * 128, n * 128)
                    fslab["v"] = load_slab("up", 0, 8, DFF + jj * 128, n * 128)
                w = slice((jj % 4) * 128, (jj % 4 + 1) * 128)
                psg = c.psum()
                for kc in range(8):
                    c.mm(psg[:, 0:2 + TB], fslab["g"][:, kc, w], hT[:, kc, HX - 2:HX + TB], start=(kc == 0), stop=(kc == 7))
                psv = c.psum()
                for kc in range(8):
                    c.mm(psv[:, 0:2 + TB], fslab["v"][:, kc, w], hT[:, kc, HX - 2:HX + TB], start=(kc == 0), stop=(kc == 7))
                stt_["g"] = {"ps": psg}
                stt_["v"] = {"ps": psv}

            cg = conv_stages(3, None, "cwf", "cbf", lambda jj: jj, None)[0]
            cv = conv_stages(3, None, "cwf", "cbf", lambda jj: NJ + jj, None)[0]

            def f1(jj, stt_):
                cg(jj, stt_["g"])
                cv(jj, stt_["v"])

            def f2(jj, stt_):
                gl = TF()[:, 0:TB]
                c.act(gl, stt_["g"]["acc"], AF.Gelu_apprx_tanh)
                c.tt(mT[:, jj, :], gl, stt_["v"]["acc"], ALU.mult, eng="pool")

            pipeline(NJ, [f0, f1, f2])

            ck(10)
            tm_norm_residual("dn", lambda k, i: mT[:, k, i * 128:(i + 1) * 128], NJ, G2W)
            c.dma("pool", y_d[b, t0:t0 + TB, :].rearrange("(i p) f -> p i f", p=128), xres[:])

    c.finish("pool", xb)
    c.finish("sp", xb)
    return nc, c


_CACHE = {}


def _get_nc(nseq, nblk):
    key = (nseq, nblk)
    if key not in _CACHE:
        _CACHE[key] = build_nc(nseq, nblk)[0]
    return _CACHE[key]


def kernel(**inputs):
    inp = {k: np.asarray(v) for k, v in inputs.items()}
    x = inp["x"].astype(np.float32, copy=False)
    B = x.shape[0]
    nseq = B // NCORES
    nblk = x.shape[1] // TB
    P, R = _host_params(inp)
    C = _host_consts()
    shared = {
        "w_ada": np.ascontiguousarray(inp["w_ada"][0], dtype=np.float32),
        "w_in": np.ascontiguousarray(inp["w_in"][0], dtype=np.float32),
        "w_ssd_o": np.ascontiguousarray(inp["w_ssd_o"][0], dtype=np.float32),
        "w_attn_o": np.ascontiguousarray(inp["w_attn_o"][0], dtype=np.float32),
        "w_out": np.ascontiguousarray(inp["w_out"][0], dtype=np.float32),
        "w_up": np.ascontiguousarray(inp["w_up"][0], dtype=np.float32),
        "w_down": np.ascontiguousarray(inp["w_down"][0], dtype=np.float32),
        "params": P, "rows": R, "consts": C,
    }
    in_maps = []
    for ci in range(NCORES):
        cs = inp["c"][ci * nseq:(ci + 1) * nseq].astype(np.float32)
        cT = np.ascontiguousarray(cs.reshape(nseq, 8, 128).transpose(2, 1, 0))
        m = dict(shared)
        m["x"] = np.ascontiguousarray(x[ci * nseq:(ci + 1) * nseq])
        m["cT"] = cT
        in_maps.append(m)
    nc = _get_nc(nseq, nblk)
    res = run_bass_kernel_spmd(nc, in_maps, core_ids=list(range(NCORES)))
    out = np.concatenate([np.asarray(r["y"]) for r in res.results], axis=0)
    return out.astype(np.float32, copy=False)
```

```python
import math
import numpy as np
import concourse.bass as bass
import concourse.mybir as mybir
from concourse.bass_utils import run_bass_kernel_spmd

F32 = mybir.dt.float32
BF16 = mybir.dt.bfloat16
I32 = mybir.dt.int32
ALU = mybir.AluOpType
AF = mybir.ActivationFunctionType
AX = mybir.AxisListType

NCORES = 8
D = 1024
SEQ = 2048
TB = 256
NT = TB // 128
DFF = 2816
NJ = DFF // 128
INW = 7696
C_Z, C_X, C_B, C_C, C_DT, C_Q, C_K, C_V, C_GS, C_GA = 0, 1024, 2048, 2304, 2560, 2576, 3600, 4624, 5648, 6672
ROPE_THETA = 500000.0
PI = math.pi


class View:
    __slots__ = ("buf", "ap")

    def __init__(self, buf, ap):
        self.buf = buf
        self.ap = ap


class Buf:
    def __init__(self, ctx, name, shape, dtype, space="sbuf"):
        self.ctx = ctx
        self.name = name
        self.space = space
        ctx.allbufs.append(self)
        nc = ctx.nc
        if space == "sbuf":
            self.t = nc.alloc_sbuf_tensor(name, list(shape), dtype)
        elif space == "psum":
            self.t = nc.alloc_psum_tensor(name, list(shape), dtype)
        else:
            self.t = nc.dram_tensor(name, list(shape), dtype)
        self.wr = {}
        self.rd = {}
        self.dkey = None
        self.dcnt = 0
        self.aliases = []

    def __getitem__(self, idx):
        return View(self, self.t[idx])

    def v(self, ap):
        return View(self, ap)


class VBuf(Buf):
    def __init__(self, ctx, name, ap, aliases=()):
        self.ctx = ctx
        self.name = name
        self.space = "sbuf"
        ctx.allbufs.append(self)
        self.t = ap
        self.wr = {}
        self.rd = {}
        self.dkey = None
        self.dcnt = 0
        self.aliases = list(aliases)


def _mx(d, k, v):
    if d.get(k, 0) < v:
        d[k] = v


class Ctx:
    ENG = ("pe", "act", "dve", "pool", "sp")

    def __init__(self, nc):
        self.nc = nc
        self.e = {"pe": nc.tensor, "act": nc.scalar, "dve": nc.vector,
                  "pool": nc.gpsimd, "sp": nc.sync}
        self.sems = {}
        self.ekey = {}
        self.ecnt = {}
        self.seen = {k: {} for k in self.ENG}
        for k in self.ENG:
            key = "e_" + k
            self.sems[key] = nc.alloc_semaphore(key)
            self.ekey[k] = key
            self.ecnt[k] = 0
        self.ninstr = 0
        self.nwaits = 0
        self.allbufs = []
        self.nop = {}
        self.marks = []
        self.dcnt = {}
        self._ps = []
        self._psi = 0

    def buf(self, name, shape, dtype, space="sbuf"):
        return Buf(self, name, shape, dtype, space)

    def _wait(self, eng, need):
        E = self.e[eng]
        seen = self.seen[eng]
        for k, v in need.items():
            if seen.get(k, 0) < v:
                E.wait_ge(self.sems[k], v)
                seen[k] = v
                self.nwaits += 1

    def op(self, eng, emit, reads=(), writes=(), signal=True):
        need = {}
        own = self.ekey[eng]
        rb = [r.buf if isinstance(r, View) else r for r in reads]
        wb = [w.buf if isinstance(w, View) else w for w in writes]
        for b in rb:
            for k, v in b.wr.items():
                _mx(need, k, v)
            if b.space == "psum":
                for k, v in b.rd.items():
                    if k != own:
                        _mx(need, k, v)
        for b0 in wb:
            for b in [b0] + b0.aliases:
                for k, v in b.wr.items():
                    if k != own or eng != "pe":
                        _mx(need, k, v)
                for k, v in b.rd.items():
                    if k != own or eng != "pe":
                        _mx(need, k, v)
        for b0 in rb:
            for b in b0.aliases:
                for k, v in b.wr.items():
                    _mx(need, k, v)
        self._wait(eng, need)
        ins = emit(self.e[eng])
        self.ninstr += 1
        self.nop[eng] = self.nop.get(eng, 0) + 1
        if signal:
            self.ecnt[eng] += 1
            ins.then_inc(self.sems[own], 1)
            t = self.ecnt[eng]
        else:
            t = self.ecnt[eng] + 1
        for b in rb:
            _mx(b.rd, own, t)
        for b in wb:
            _mx(b.wr, own, t)
        return ins

    def dma(self, eng, out, in_, **kw):
        need = {}
        tracked = []
        if isinstance(out, View):
            b = out.buf
            for d in (b.wr, b.rd):
                for k, v in d.items():
                    _mx(need, k, v)
            tracked.append((b, "w"))
            oap = out.ap
        else:
            oap = out
        if isinstance(in_, View):
            b = in_.buf
            for k, v in b.wr.items():
                _mx(need, k, v)
            tracked.append((b, "r"))
            iap = in_.ap
        else:
            iap = in_
        self._wait(eng, need)
        b0 = tracked[0][0]
        dkey = "d_%s_%s" % (eng, b0.name)
        if dkey not in self.sems:
            self.sems[dkey] = self.nc.alloc_semaphore(dkey)
            self.dcnt[dkey] = 0
        self.dcnt[dkey] += 16
        ins = self.e[eng].dma_start(out=oap, in_=iap, **kw)
        ins.then_inc(self.sems[dkey], 16)
        self.ninstr += 1
        for b, m in tracked:
            if m == "w":
                _mx(b.wr, dkey, self.dcnt[dkey])
            else:
                _mx(b.rd, dkey, self.dcnt[dkey])
        return ins

    def finish(self, eng, bufs):
        need = {}
        for b in bufs:
            for d in (b.wr, b.rd):
                for k, v in d.items():
                    _mx(need, k, v)
        self._wait(eng, need)

    def psum(self, hold=False):
        n = len(self._ps)
        for _ in range(n):
            b = self._ps[self._psi % n]
            self._psi += 1
            if not getattr(b, "held", False):
                break
        else:
            raise RuntimeError("all PSUM banks held")
        b.held = hold
        return b

    def release(self, b):
        b.held = False

    def mm(self, out, lhsT, rhs, start=True, stop=True, signal=None, serialize=False):
        if signal is None:
            signal = stop
        if serialize:
            own = self.ekey["pe"]
            v = out.buf.wr.get(own, 0)
            if v and self.seen["pe"].get(own, 0) < v:
                self.e["pe"].wait_ge(self.sems[own], v)
                self.seen["pe"][own] = v
        return self.op("pe", lambda E: E.matmul(out.ap, lhsT.ap, rhs.ap, start=start, stop=stop),
                       [lhsT, rhs], [out], signal=signal)

    def tr(self, out, in_, ident, signal=True):
        return self.op("pe", lambda E: E.transpose(out.ap, in_.ap, ident.ap), [in_, ident], [out],
                       signal=signal)

    def act(self, out, in_, func, scale=1.0, bias=None, accum=None, eng="act"):
        reads = [in_]
        writes = [out]
        kw = {}
        if isinstance(scale, View):
            reads.append(scale)
            kw["scale"] = scale.ap
        else:
            kw["scale"] = float(scale)
        if bias is not None:
            if isinstance(bias, View):
                reads.append(bias)
                kw["bias"] = bias.ap
            else:
                kw["bias"] = float(bias)
        if accum is not None:
            writes.append(accum)
            kw["accum_out"] = accum.ap
        return self.op(eng, lambda E: E.activation(out.ap, in_.ap, func, **kw), reads, writes)

    def tt(self, out, in0, in1, op, eng="dve"):
        return self.op(eng, lambda E: E.tensor_tensor(out.ap, in0.ap, in1.ap, op), [in0, in1], [out])

    def ts(self, out, in0, s1, s2=None, op0=ALU.mult, op1=None, eng="dve"):
        reads = [in0]
        a1 = s1
        a2 = s2
        if isinstance(s1, View):
            reads.append(s1)
            a1 = s1.ap
        if isinstance(s2, View):
            reads.append(s2)
            a2 = s2.ap
        if op1 is None:
            return self.op(eng, lambda E: E.tensor_scalar(out.ap, in0.ap, a1, a2, op0), reads, [out])
        return self.op(eng, lambda E: E.tensor_scalar(out.ap, in0.ap, a1, a2, op0, op1), reads, [out])

    def stt(self, out, in0, s, in1, op0, op1, eng="dve"):
        reads = [in0, in1]
        a = s
        if isinstance(s, View):
            reads.append(s)
            a = s.ap
        return self.op(eng, lambda E: E.scalar_tensor_tensor(out.ap, in0.ap, a, in1.ap, op0, op1),
                       reads, [out])

    def copy(self, out, in_, eng="dve"):
        return self.op(eng, lambda E: E.tensor_copy(out.ap, in_.ap), [in_], [out])

    def memset(self, out, val, eng="dve"):
        return self.op(eng, lambda E: E.memset(out.ap, val), [], [out])

    def recip(self, out, in_):
        return self.op("dve", lambda E: E.reciprocal(out.ap, in_.ap), [in_], [out])


def _param_layout():
    items = [("pn1", 8), ("pn2", 8), ("cws", 48), ("cbs", 12), ("cwf", 132), ("cbf", 44),
             ("subw", 1), ("bada", 48), ("invf", 1), ("dtb", 16), ("alog", 16), ("dsk", 16),
             ("lam", 256), ("snw", 1024)]
    off = {}
    o = 0
    for n, w in items:
        off[n] = (o, w)
        o += w
    return off, o


PO, NP = _param_layout()
CO = {"ident": (0, 128), "U": (128, 128), "tri": (256, 128), "sigT": (384, 128), "iota": (512, 256),
      "mask0": (768, 256), "mask1": (1024, 256)}
NCST = 1280
RO = {"bada": (0, 6144), "post1": (6144, 1024), "post2": (7168, 1024)}
NROW = 8192


def _col(v, n):
    return np.ascontiguousarray(np.asarray(v, np.float32).reshape(n, 128).T)


def _rep(v):
    v = np.asarray(v, np.float32).reshape(1, -1)
    return np.repeat(v, 128, axis=0)


def _host_params(inp):
    P = np.zeros((128, NP), np.float32)

    def put(name, arr):
        o, w = PO[name]
        P[:, o:o + w] = np.asarray(arr, np.float32).reshape(128, w)

    put("pn1", _col(inp["pre_norm1_w"][0], 8))
    put("pn2", _col(inp["pre_norm2_w"][0], 8))
    put("cws", np.asarray(inp["conv_ssd_w"][0]).reshape(4, 12, 128).transpose(2, 1, 0))
    put("cbs", _col(inp["conv_ssd_b"][0], 12))
    put("cwf", np.asarray(inp["conv_ffn_w"][0]).reshape(3, 44, 128).transpose(2, 1, 0))
    put("cbf", _col(inp["conv_ffn_b"][0], 44))
    put("subw", np.asarray(inp["subln_w"][0]).reshape(128, 1))
    put("bada", _col(inp["b_ada"][0], 48))
    invf = np.zeros((128, 1), np.float32)
    for p in range(128):
        d = p % 64
        if d < 16:
            invf[p, 0] = np.float32(np.power(np.float32(ROPE_THETA), np.float32(-(d % 8) / 8.0)))
    put("invf", invf)
    put("dtb", _rep(inp["dt_bias"][0]))
    put("alog", _rep(inp["a_log"][0]))
    put("dsk", _rep(inp["d_skip"][0]))
    put("lam", _rep(np.concatenate([inp["lambda_q1"][0], inp["lambda_k1"][0],
                                    inp["lambda_q2"][0], inp["lambda_k2"][0]])))
    put("snw", _rep(inp["ssd_norm_w"][0]))
    R = np.zeros((1, NROW), np.float32)
    R[0, 0:6144] = inp["b_ada"][0]
    R[0, 6144:7168] = inp["post_norm1_w"][0]
    R[0, 7168:8192] = inp["post_norm2_w"][0]
    return P, R


def _host_consts():
    C = np.zeros((128, NCST), np.float32)
    i = np.arange(128)
    C[:, 0:128] = np.eye(128, dtype=np.float32)
    C[:, 128:256] = (i[:, None] > i[None, :]).astype(np.float32)
    tri = (i[:, None] <= i[None, :]).astype(np.float32)
    C[:, 256:384] = tri
    sig = np.zeros((128, 128), np.float32)
    for cb in (0, 64):
        for d in range(8):
            sig[cb + d + 8, cb + d] = -1.0
            sig[cb + d, cb + d + 8] = 1.0
    C[:, 384:512] = sig
    C[:, 512:768] = np.arange(256, dtype=np.float32)[None, :]
    C[:, 768:896] = tri
    C[:, 896:1024] = 1.0
    C[:, 1024:1152] = 0.0
    C[:, 1152:1280] = tri
    return C


class _Stop(Exception):
    pass


STOP = [0]


def build_nc(NSEQ=4, NBLK=8):
    try:
        return _build_nc(NSEQ, NBLK)
    except _Stop as e:
        nc, c = e.args
        for eng in ("sp", "pool", "act", "dve", "pe"):
            c.finish(eng, c.allbufs)
        return nc, c


def _build_nc(NSEQ=4, NBLK=8):
    nc = bass.Bass("TRN2", target_bir_lowering=False)
    T = NBLK * TB
    x_d = nc.dram_tensor("x", [NSEQ, T, D], F32, kind="ExternalInput").ap()
    cT_d = nc.dram_tensor("cT", [128, 8, NSEQ], F32, kind="ExternalInput").ap()
    wada_d = nc.dram_tensor("w_ada", [D, 6 * D], F32, kind="ExternalInput").ap()
    win_d = nc.dram_tensor("w_in", [D, INW], F32, kind="ExternalInput").ap()
    wso_d = nc.dram_tensor("w_ssd_o", [D, D], F32, kind="ExternalInput").ap()
    wao_d = nc.dram_tensor("w_attn_o", [D, D], F32, kind="ExternalInput").ap()
    wout_d = nc.dram_tensor("w_out", [D, D], F32, kind="ExternalInput").ap()
    wup_d = nc.dram_tensor("w_up", [D, 2 * DFF], F32, kind="ExternalInput").ap()
    wdn_d = nc.dram_tensor("w_down", [DFF, D], F32, kind="ExternalInput").ap()
    par_d = nc.dram_tensor("params", [128, NP], F32, kind="ExternalInput").ap()
    row_d = nc.dram_tensor("rows", [1, NROW], F32, kind="ExternalInput").ap()
    cst_d = nc.dram_tensor("consts", [128, NCST], F32, kind="ExternalInput").ap()
    y_d = nc.dram_tensor("y", [NSEQ, T, D], F32, kind="ExternalOutput").ap()

    c = Ctx(nc)

    def ck(k):
        c.marks.append((k, dict(c.nop)))
        if STOP[0] == k:
            raise _Stop(nc, c)

    c._ps = [c.buf("ps%d" % i, [128, 512], F32, "psum") for i in range(7)]

    def psacc():
        return c.psum()

    tpb = [c.buf("tp%d" % i, [128, 8, 128], BF16, "psum") for i in range(1)]
    tpi = [0]

    def tpbank():
        b = tpb[tpi[0] % len(tpb)]
        tpi[0] += 1
        return b

    cst = c.buf("cst", [128, 512], F32)
    par = c.buf("par", [128, NP], F32)
    c.dma("sp", cst[:, 0:256], cst_d[:, 128:384])
    c.dma("sp", cst[:, 256:512], cst_d[:, 512:768])
    c.dma("sp", par[:], par_d)

    def P(name, a=None, b=None):
        o, w = PO[name]
        lo = o if a is None else o + a
        hi = o + w if b is None else o + b
        return par[:, lo:hi]

    def CS(name):
        o, w = {"U": (0, 128), "tri": (128, 128), "iota": (256, 256)}[name]
        return cst[:, o:o + w]

    identb = c.buf("identb", [128, 128], BF16)
    trib = c.buf("trib", [128, 128], BF16)
    sigTb = c.buf("sigTb", [128, 128], BF16)
    onesb = c.buf("onesb", [128, 128], BF16)
    onesf = c.buf("onesf", [128, 128], F32)
    maskb = c.buf("maskb", [128, 2, 256], BF16)
    c.dma("pool", identb[:], cst_d[:, 0:128])
    c.dma("pool", trib[:], cst_d[:, 256:384])
    c.dma("pool", sigTb[:], cst_d[:, 384:512])
    c.memset(onesb[:], 1.0)
    c.memset(onesf[:], 1.0)
    c.dma("pool", maskb[:, 0, :], cst_d[:, 768:1024])
    c.dma("pool", maskb[:, 1, :], cst_d[:, 1024:1280])
    Uf = CS("U")
    trif = CS("tri")
    iota = CS("iota")

    Wb = {}
    for name, ap, KC, N in (("in", win_d, 8, INW), ("so", wso_d, 8, D), ("ao", wao_d, 8, D),
                            ("out", wout_d, 8, D), ("up", wup_d, 8, 2 * DFF), ("dn", wdn_d, NJ, D)):
        Wb[name] = c.buf("wb_" + name, [128, KC, N], BF16, "dram")
        src = ap.rearrange("(kc p) n -> p kc n", p=128)
        for kc in range(KC):
            c.dma("pool", Wb[name][:, kc, :], src[:, kc, :])
    wada_src = wada_d.rearrange("(kc p) n -> p kc n", p=128)

    NSLAB = 4
    slabs = [c.buf("slab%d" % i, [128, 8, 512], BF16) for i in range(NSLAB)]
    sli = [0]

    def load_slab(name, k0, nk, col0, ncols):
        s = slabs[sli[0] % NSLAB]
        sli[0] += 1
        c.dma("sp", s[:, 0:nk, 0:ncols], Wb[name][:, k0:k0 + nk, col0:col0 + ncols])
        return s

    def load_slab_ada(col0, ncols):
        s = slabs[sli[0] % NSLAB]
        sli[0] += 1
        c.dma("pool", s[:, 0:8, 0:ncols], wada_src[:, :, col0:col0 + ncols])
        return s

    Wdt = c.buf("Wdt", [128, 8, 16], BF16)
    c.dma("sp", Wdt[:], Wb["in"][:, :, C_DT:C_DT + 16])

    sm = c.buf("sm", [128, 64], F32)
    lamt = c.buf("lamt", [128, 128], F32)
    c.tt(lamt[:, 0:64], P("lam", 0, 64), P("lam", 64, 128), ALU.mult)
    c.tt(lamt[:, 64:128], P("lam", 128, 192), P("lam", 192, 256), ALU.mult)
    c.op("dve", lambda E: E.reduce_sum(sm[:, 0:1].ap, lamt[:, 0:64].ap, axis=AX.X), [lamt], [sm])
    c.op("dve", lambda E: E.reduce_sum(sm[:, 1:2].ap, lamt[:, 64:128].ap, axis=AX.X), [lamt], [sm])
    c.act(sm[:, 2:4], sm[:, 0:2], AF.Exp)
    c.tt(sm[:, 4:5], sm[:, 3:4], sm[:, 2:3], ALU.subtract)
    c.ts(sm[:, 5:6], sm[:, 4:5], -0.2, None, ALU.add)
    nlam = sm[:, 5:6]
    c.ts(sm[:, 6:7], P("subw"), 0.8, None, ALU.mult)
    subw = sm[:, 6:7]
    arow = c.buf("arow", [128, 16], F32)
    c.act(arow[:], P("alog"), AF.Exp)
    c.ts(arow[:], arow[:], -1.0, None, ALU.mult)

    cTs = c.buf("cTs", [128, 8, NSEQ], F32)
    c.dma("sp", cTs[:], cT_d)
    scT = c.buf("scT", [128, 8, NSEQ], F32)
    c.act(scT[:], cTs[:], AF.Silu)
    scTb = c.buf("scTb", [128, 8, NSEQ], BF16)
    c.copy(scTb[:], scT[:])
    modT = c.buf("modT", [128, 32, NSEQ], F32)
    mod_cols = [0, 1024, 3072, 4096]
    psm = c.psum()
    for gi, col0 in enumerate(mod_cols):
        for half in range(2):
            s = load_slab_ada(col0 + half * 512, 512)
            for cc in range(4):
                ci = gi * 8 + half * 4 + cc
                for kc in range(8):
                    c.mm(psm[:, ci * NSEQ:(ci + 1) * NSEQ], s[:, kc, cc * 128:(cc + 1) * 128],
                         scTb[:, kc, :], start=(kc == 0), stop=(kc == 7))
    bo = PO["bada"][0]
    for gi, col0 in enumerate(mod_cols):
        fc = col0 // 128
        c.tt(modT[:, gi * 8:(gi + 1) * 8, :],
             psm.v(psm.t[:, gi * 8 * NSEQ:(gi + 1) * 8 * NSEQ].rearrange("p (a b) -> p a b", b=NSEQ)),
             par.v(par.t[:, bo + fc:bo + fc + 8].unsqueeze(2).to_broadcast([128, 8, NSEQ])), ALU.add)
    A1 = c.buf("A1", [128, 8, NSEQ], F32)
    A2 = c.buf("A2", [128, 8, NSEQ], F32)
    for Ab, pn, sc0 in ((A1, "pn1", 8), (A2, "pn2", 24)):
        o, w = PO[pn]
        c.ts(Ab[:], modT[:, sc0:sc0 + 8, :], 1.0, None, ALU.add)
        c.tt(Ab[:], Ab[:], par.v(par.t[:, o:o + 8].unsqueeze(2).to_broadcast([128, 8, NSEQ])), ALU.mult)

    ck(1)
    KT = c.buf("KT", [128, 8, T], BF16)
    Vt = c.buf("Vt", [128, T // 128, D], BF16)
    xb = [c.buf("xres0", [128, NT, D], F32)]
    HX = 3
    hT = c.buf("hT", [128, 8, HX + TB], BF16)
    h1halo = c.buf("h1halo", [128, 8, 3], BF16)
    h2halo = c.buf("h2halo", [128, 8, 2], BF16)
    qT = c.buf("qT", [128, 8, TB], BF16)
    yaT = c.buf("yaT", [128, 8, TB], BF16)
    ysT = c.buf("ysT", [128, 8, TB], BF16)
    mgT = qT
    xact = c.buf("xact", [128, 12, TB], BF16)
    zs = c.buf("zs", [128, NT, D], BF16)
    G1W = c.buf("G1W", [128, D], F32)
    G2W = c.buf("G2W", [128, D], F32)
    Sst = c.buf("Sst", [128, D], F32)
    Sbf = c.buf("Sbf", [128, D], BF16)
    cosb = c.buf("cosb", [128, TB], F32)
    sinb = c.buf("sinb", [128, TB], F32)
    scr = c.buf("scr", [128, 4096], F32)
    scr_bf = scr.t[:, :].bitcast(BF16)
    Rr = VBuf(c, "Rr", scr.t[:, 0:2048].rearrange("p (h l) -> p h l", h=16))
    Ee = VBuf(c, "Ee", scr_bf[:, 4096:6144].rearrange("p (h l) -> p h l", h=16))
    Mm = Ee
    ybuf = VBuf(c, "ybuf", scr.t[:, 3072:4096])
    mT = VBuf(c, "mT", scr_bf[:, 0:NJ * TB].rearrange("p (j t) -> p j t", j=NJ), aliases=[Rr, Ee])
    m1all = c.buf("m1all", [128, 4 * TB], F32)
    m1b = [VBuf(c, "m1b%d" % i, m1all.t[:, i * TB:(i + 1) * TB]) for i in range(4)]
    PaccV = [VBuf(c, "pacc%d" % i, m1all.t[:, 2 * i * TB:(2 * i + 2) * TB],
                  aliases=[m1b[2 * i], m1b[2 * i + 1]]) for i in range(2)]
    for i in range(4):
        m1b[i].aliases = [PaccV[i // 2]]
    Pb = c.buf("Pb", [128, 2 * TB], BF16)
    Rr.aliases = [mT]
    Ee.aliases = [mT]
    ytmp = c.buf("ytmp", [128, D], F32)
    junk = VBuf(c, "junk", ytmp.t[:, 0:512].bitcast(BF16), aliases=[ytmp])
    ytmp.aliases = [junk]
    xn = c.buf("xn", [128, D], BF16)
    ytok = xn
    ssq = c.buf("ssq", [128, 16], F32)
    qraw = [c.buf("qraw%d" % i, [128, TB], BF16) for i in range(2)]
    ptile = [c.buf("pt%d" % i, [128, 2 * TB], BF16) for i in range(4)]
    tfp = [c.buf("tf%d" % i, [128, TB + 4], F32) for i in range(8)]
    tfi = [0]

    def TF():
        bb = tfp[tfi[0] % len(tfp)]
        tfi[0] += 1
        return bb

    dtv = c.buf("dtv", [128, NT, 16], F32)
    lav = c.buf("lav", [128, NT, 16], F32)
    dtt = c.buf("dtt", [128, 16], F32)
    xs_tok = c.buf("xs_tok", [128, D], BF16)
    B_tok = c.buf("B_tok", [128, 2, 128], BF16)
    xd = c.buf("xd", [128, 16, 64], BF16)
    xdd = c.buf("xdd", [128, 16, 64], BF16)
    Gm = c.buf("Gm", [128, 2, 128], BF16)
    sml = c.buf("sml", [128, 80], F32)
    onesrow = onesf[0:1, :]

    def rstd_from_ss(dst, ss, n, eps):
        c.act(dst, ss, AF.Ln, scale=1.0 / n, bias=eps)
        c.act(dst, dst, AF.Exp, scale=-0.5)

    def norm_to_T(xsrc, Ab, boff, b, hbuf, H):
        for i in range(NT):
            c.act(junk[:], xsrc[:, i, :], AF.Square, accum=ssq[:, i:i + 1])
            rstd_from_ss(ssq[:, 4 + i:5 + i], ssq[:, i:i + 1], D, 1e-6)
            c.ts(xn[:], xsrc[:, i, :], ssq[:, 4 + i:5 + i], None, ALU.mult)
            tp = tpbank()
            for kc in range(8):
                c.tr(tp[:, kc, :], xn[:, kc * 128:(kc + 1) * 128], identb[:], signal=(kc == 7))
            for kc in range(8):
                c.act(hT[:, kc, HX + i * 128:HX + (i + 1) * 128], tp[:, kc, :], AF.Identity,
                      scale=Ab[:, kc, b:b + 1], bias=modT[:, boff + kc, b:b + 1])
        c.copy(hT[:, :, HX - H:HX], hbuf[:], eng="pool")
        c.copy(hbuf[:], hT[:, :, HX + TB - H:HX + TB], eng="pool")

    def pipeline(n, stages):
        ns = len(stages)
        st = [dict() for _ in range(n)]
        for t in range(n + ns - 1):
            for si in range(ns):
                i = t - si
                if 0 <= i < n:
                    stages[si](i, st[i])

    def proj_fm(wname, col0, nchunks, rhsT, stages, pc=0):
        slab = {}

        def s0(ci, stt_):
            if ci % 4 == 0:
                n = min(4, nchunks - ci)
                slab["s"] = load_slab(wname, 0, 8, col0 + ci * 128, n * 128)
            s = slab["s"]
            ps = c.psum()
            w = slice((ci % 4) * 128, (ci % 4 + 1) * 128)
            for kc in range(8):
                c.mm(ps[:, 0:TB + pc], s[:, kc, w], rhsT[:, kc, HX - pc:HX + TB], start=(kc == 0), stop=(kc == 7))
            stt_["ps"] = ps

        pipeline(nchunks, [s0] + list(stages))

    def proj_tm(wname, col0, nslab, lhsT_buf, consume):
        slab = {}

        def s0(it, stt_):
            hf, i = divmod(it, NT)
            if i == 0:
                slab["s"] = load_slab(wname, 0, 8, col0 + hf * 512, 512)
            s = slab["s"]
            ps = c.psum()
            for kc in range(8):
                c.mm(ps[:, :], lhsT_buf[:, kc, HX + i * 128:HX + (i + 1) * 128], s[:, kc, :],
                     start=(kc == 0), stop=(kc == 7))
            stt_["ps"] = ps

        def s1(it, stt_):
            hf, i = divmod(it, NT)
            consume(hf, i, stt_["ps"])

        pipeline(nslab * NT, [s0, s1])

    def range_reduce_sin(dst, shift, ang, angk, angf, angm):
        src = ang
        if shift != 0.0:
            sB = TF()
            src = sB.v(sB.t[:, 0:TB])
            c.ts(src, ang, shift, None, ALU.add)
        c.ts(angk, src, 1.0 / (2 * PI), None, ALU.mult)
        c.copy(angm, angk)
        c.stt(angm, angm, -2 * PI, src, ALU.mult, ALU.add)
        c.ts(angf, angm, PI, 2 * PI, ALU.is_gt, ALU.mult)
        c.tt(angm, angm, angf, ALU.subtract)
        c.ts(angf, angm, -PI, 2 * PI, ALU.is_lt, ALU.mult)
        c.tt(angm, angm, angf, ALU.add)
        c.act(dst[:], angm, AF.Sin)

    for b in range(NSEQ):
        c.memset(Sst[:], 0.0)
        c.memset(Sbf[:], 0.0)
        c.memset(h1halo[:], 0.0)
        c.memset(h2halo[:], 0.0)
        screp = xn.v(xn.t[:, :].rearrange("p (a b) -> p a b", a=8))
        c.copy(screp, scT.v(scT.t[:, :, b:b + 1].to_broadcast([128, 8, 128])))
        for Gw, gcol, prow in ((G1W, 2048, "post1"), (G2W, 5120, "post2")):
            for half in range(2):
                s = load_slab_ada(gcol + half * 512, 512)
                rb = ytmp[0:1, 0:512]
                ro = RO["bada"][0] + gcol + half * 512
                c.dma("sp", rb, row_d[:, ro:ro + 512])
                psA = c.psum()
                for kc in range(8):
                    c.mm(psA[:, :], xn.v(xn.t[:, kc * 128:(kc + 1) * 128]), s[:, kc, :], start=(kc == 0), stop=False)
                c.mm(psA[:, :], onesrow, rb, start=False, stop=True)
                rb2 = ytmp[0:1, 512:1024]
                ro2 = RO[prow][0] + half * 512
                c.dma("sp", rb2, row_d[:, ro2:ro2 + 512])
                psB = c.psum()
                c.mm(psB[:, :], onesrow, rb2, start=True, stop=True)
                gtmp = ybuf[:, 0:512]
                c.act(gtmp, psB[:, :], AF.Identity)
                c.tt(Gw[:, half * 512:(half + 1) * 512], psA[:, :], gtmp, ALU.mult)

        ck(2)
        for blk in range(NBLK):
            t0 = blk * TB
            xres = xb[0]
            c.dma("pool", xres[:], x_d[b, t0:t0 + TB, :].rearrange("(i p) f -> p i f", p=128))
            angB, angkB, angfB, angmB = TF(), TF(), TF(), TF()
            ang = angB.v(angB.t[:, 0:TB])
            angk = angkB.v(angkB.t[:, 0:TB].bitcast(I32))
            angf = angfB.v(angfB.t[:, 0:TB])
            angm = angmB.v(angmB.t[:, 0:TB])
            c.ts(ang, iota, float(t0), None, ALU.add)
            c.ts(ang, ang, P("invf"), None, ALU.mult)
            range_reduce_sin(sinb, 0.0, ang, angk, angf, angm)
            range_reduce_sin(cosb, PI / 2, ang, angk, angf, angm)
            norm_to_T(xres, A1, 0, b, h1halo, 3)
            ck(3)

            def rope_stages(dst_fn):
                def r1(ci, stt_):
                    ps = stt_["ps"]
                    qr = qraw[ci % 2]
                    c.act(qr[:], ps[:, 0:TB], AF.Identity)
                    ps2 = c.psum()
                    c.mm(ps2[:, 0:TB], sigTb[:], qr[:])
                    t1 = TF()
                    c.tt(t1[:, 0:TB], ps[:, 0:TB], cosb[:], ALU.mult)
                    stt_["ps2"] = ps2
                    stt_["t1"] = t1

                def r2(ci, stt_):
                    t2 = TF()
                    c.tt(t2[:, 0:TB], stt_["ps2"][:, 0:TB], sinb[:], ALU.mult)
                    c.tt(dst_fn(ci), stt_["t1"][:, 0:TB], t2[:, 0:TB], ALU.add, eng="pool")
                return [r1, r2]

            proj_fm("in", C_K, 8, hT, rope_stages(lambda ci: KT[:, ci, t0:t0 + TB]))
            proj_fm("in", C_Q, 8, hT, rope_stages(lambda ci: qT[:, ci, :]))
            proj_tm("in", C_V, 2, hT, lambda hf, i, ps: c.act(
                Vt[:, blk * NT + i, hf * 512:(hf + 1) * 512], ps[:, :], AF.Identity))

            ck(4)
            def conv_stages(K, halo, wname, bname, chan_fn, fin):
                H = K - 1

                def c1(ci, stt_):
                    ps = stt_["ps"]
                    ch = chan_fn(ci)
                    acc = TF()[:, 0:TB]
                    wo = PO[wname][0] + ch * K
                    bo2 = PO[bname][0] + ch
                    c.act(acc, ps[:, H:H + TB], AF.Identity, scale=par[:, wo + H:wo + H + 1],
                          bias=par[:, bo2:bo2 + 1])
                    for k in range(H):
                        c.stt(acc, ps[:, k:k + TB], par[:, wo + k:wo + k + 1], acc, ALU.mult, ALU.add)
                    stt_["acc"] = acc

                def c2(ci, stt_):
                    fin(ci, stt_)
                return [c1, c2]

            for i in range(NT):
                ps = c.psum()
                for kc in range(8):
                    c.mm(ps[:, 0:16], hT[:, kc, HX + i * 128:HX + (i + 1) * 128], Wdt[:, kc, :],
                         start=(kc == 0), stop=(kc == 7))
                c.tt(dtt[:], ps[:, 0:16], P("dtb"), ALU.add)
                c.act(dtt[:], dtt[:], AF.Exp)
                c.act(dtv[:, i, :], dtt[:], AF.Ln, bias=1.0)
                c.tt(lav[:, i, :], dtv[:, i, :], arow[:], ALU.mult)
            proj_fm("in", C_X, 12, hT, conv_stages(
                4, None, "cws", "cbs", lambda ci: ci,
                lambda ci, stt_: c.act(xact[:, ci, :], stt_["acc"], AF.Silu)), pc=3)
            proj_tm("in", C_Z, 2, hT, lambda hf, i, ps: c.act(
                zs[:, i, hf * 512:(hf + 1) * 512], ps[:, :], AF.Silu))

            def ssd_gen():
                for i in range(NT):
                    cols = slice(i * 128, (i + 1) * 128)
                    tp = tpbank()
                    for c8 in range(8):
                        c.tr(tp[:, c8, :], xact[:, c8, cols], identb[:], signal=(c8 == 7))
                    c.copy(xs_tok.v(xs_tok.t[:, :].rearrange("p (a b) -> p a b", a=8)), tp[:, :, :])
                    tp2 = tpbank()
                    for g in range(2):
                        c.tr(tp2[:, g, :], xact[:, 8 + g, cols], identb[:], signal=(g == 1))
                    c.copy(B_tok[:], tp2[:, 0:2, :])
                    xs3 = xs_tok.v(xs_tok.t[:, :].rearrange("p (h d) -> p h d", h=16))
                    c.tt(xd[:], xs3, dtv.v(dtv.t[:, i, :].unsqueeze(2).to_broadcast([128, 16, 64])), ALU.mult)
                    yield
                    psg = c.psum()
                    for g in range(2):
                        c.mm(psg[:, g * 128:(g + 1) * 128], xact[:, 8 + g, cols], xact[:, 10 + g, cols])
                    c.tt(Gm[:], psg.v(psg.t[:, 0:256].rearrange("p (g l) -> p g l", g=2)),
                         cst.v(trif.ap.unsqueeze(1).to_broadcast([128, 2, 128])), ALU.mult)
                    c.tt(Rr[:], lav.v(lav.t[:, i, :].unsqueeze(2).to_broadcast([128, 16, 128])),
                         cst.v(trif.ap.unsqueeze(1).to_broadcast([128, 16, 128])), ALU.mult, eng="pool")
                    yield
                    for q4 in range(4):
                        pse = c.psum()
                        c.mm(pse[:, :], Uf, Rr.v(Rr.t[:, q4 * 4:(q4 + 1) * 4, :].rearrange("p h l -> p (h l)")))
                        c.act(Ee.v(Ee.t[:, q4 * 4:(q4 + 1) * 4, :].rearrange("p h l -> p (h l)")), pse[:, :], AF.Exp)
                        g = q4 // 2
                        c.tt(Mm[:, q4 * 4:(q4 + 1) * 4, :], Ee[:, q4 * 4:(q4 + 1) * 4, :],
                             Gm.v(Gm.t[:, g:g + 1, :].to_broadcast([128, 4, 128])), ALU.mult)
                        yield
                    psc = c.psum()
                    c.mm(psc[:, 0:16], trif, lav[:, i, :], signal=False)
                    c.mm(psc[:, 16:32], onesf[:], lav[:, i, :])
                    c.copy(sml[:, 0:16], psc[:, 0:16])
                    c.tt(sml[:, 16:32], psc[:, 16:32], sml[:, 0:16], ALU.subtract)
                    c.act(sml[:, 32:48], sml[:, 16:32], AF.Exp)
                    c.act(sml[:, 48:64], sml[:, 0:16], AF.Exp)
                    c.act(sml[:, 64:80], psc[:, 16:32], AF.Exp)
                    c.tt(xdd[:], xd[:], sml.v(sml.t[:, 32:48].unsqueeze(2).to_broadcast([128, 16, 64])), ALU.mult)
                    yield
                    psY = [psacc(), psacc()]
                    for h in range(16):
                        c.mm(psY[h // 8][:, (h % 8) * 64:(h % 8 + 1) * 64], Mm[:, h, :], xd[:, h, :],
                             signal=(h % 8 == 7))
                    psOf = [psacc(), psacc()]
                    for g in range(2):
                        c.mm(psOf[g][:, :], xact[:, 10 + g, cols], Sbf[:, g * 512:(g + 1) * 512])
                    for g in range(2):
                        hs = slice(g * 512, (g + 1) * 512)
                        c.tt(ybuf.v(ybuf.t[:, hs].rearrange("p (h d) -> p h d", h=8)),
                             psOf[g].v(psOf[g].t[:, :].rearrange("p (h d) -> p h d", h=8)),
                             sml.v(sml.t[:, 48 + g * 8:56 + g * 8].unsqueeze(2).to_broadcast([128, 8, 64])),
                             ALU.mult)
                        c.tt(ybuf[:, hs], ybuf[:, hs], psY[g][:, :], ALU.add)
                    yield
                    psS = [c.psum(), c.psum()]
                    for g in range(2):
                        c.mm(psS[g][:, :], B_tok[:, g, :],
                             xdd.v(xdd.t[:, g * 8:(g + 1) * 8, :].rearrange("p h d -> p (h d)")))
                    for g in range(2):
                        hs = slice(g * 512, (g + 1) * 512)
                        c.tt(Sst.v(Sst.t[:, hs].rearrange("p (h d) -> p h d", h=8)),
                             Sst.v(Sst.t[:, hs].rearrange("p (h d) -> p h d", h=8)),
                             sml.v(sml.t[:, 64 + g * 8:72 + g * 8].unsqueeze(2).to_broadcast([128, 8, 64])),
                             ALU.mult)
                        c.tt(Sst[:, hs], Sst[:, hs], psS[g][:, :], ALU.add)
                    c.copy(Sbf[:], Sst[:], eng="pool")
                    yield
                    c.tt(ytmp.v(ytmp.t[:, :].rearrange("p (h d) -> p h d", h=16)), xs3,
                         par.v(P("dsk").ap.unsqueeze(2).to_broadcast([128, 16, 64])), ALU.mult, eng="pool")
                    c.tt(ybuf[:], ybuf[:], ytmp[:], ALU.add)
                    c.tt(ybuf[:], ybuf[:], zs[:, i, :], ALU.mult)
                    yield
                    for g in range(2):
                        hs = slice(g * 512, (g + 1) * 512)
                        c.act(junk[:, 0:512], ybuf[:, hs], AF.Square, accum=ssq[:, 8 + g:9 + g])
                        rstd_from_ss(ssq[:, 10 + g:11 + g], ssq[:, 8 + g:9 + g], 512, 1e-5)
                        so = PO["snw"][0]
                        c.stt(ytok[:, hs], ybuf[:, hs], ssq[:, 10 + g:11 + g],
                              par[:, so + g * 512:so + (g + 1) * 512], ALU.mult, ALU.mult)
                    tp = tpbank()
                    for c8 in range(8):
                        c.tr(tp[:, c8, :], ytok[:, c8 * 128:(c8 + 1) * 128], identb[:], signal=(c8 == 7))
                    c.copy(ysT[:, :, cols], tp[:, :, :])
                    yield


            ck(5)
            nkb = NT * blk + NT
            LAG = 3 if nkb >= 6 else 2

            def attn_finalize(h, psO, Pacc):
                c.copy(Pb[:], Pacc[:, :], eng="pool")
                psL = c.psum()
                c.mm(psL[:, :], onesb[:], Pb[:])
                rr, on = TF(), TF()
                r2 = ytmp[:, 0:2 * TB]
                o2 = ytmp[:, 2 * TB:4 * TB]
                c.act(r2, psL[:, :], AF.Ln)
                c.act(r2, r2, AF.Exp, scale=-1.0)
                c.tt(o2, psO[:, :], r2, ALU.mult)
                c.release(psO)
                oo = on[:, 0:TB]
                sq = rr[:, 0:TB]
                sqb = qraw[h % 2]
                c.stt(oo, ytmp[:, 3 * TB:4 * TB], nlam, ytmp[:, 2 * TB:3 * TB], ALU.mult, ALU.add)
                c.act(sqb[:], oo, AF.Square)
                pss = c.psum()
                c.mm(pss[:, 0:TB], onesb[:], sqb[:])
                c.act(sq, pss[:, 0:TB], AF.Ln, scale=1.0 / 128, bias=1e-5)
                c.act(sq, sq, AF.Exp, scale=-0.5)
                c.stt(yaT[:, h, :], oo, subw, sq, ALU.mult, ALU.mult)

            sgen = ssd_gen()
            n_iter = 8 * (nkb + LAG)
            stride = max(1, n_iter // 26)
            it_cnt = [0]

            def ssd_step():
                it_cnt[0] += 1
                if it_cnt[0] % stride == 0:
                    next(sgen, None)

            pend = None
            for h in range(8):
                psO = c.psum(hold=True)
                Pacc = PaccV[h % 2]
                pts = {}
                for kb in range(nkb + LAG):
                    if kb < nkb:
                        pss = c.psum()
                        c.mm(pss[:, 0:TB], KT[0:64, h, kb * 128:(kb + 1) * 128], qT[0:64, h, :],
                             signal=(kb < LAG))
                    if kb >= LAG:
                        k2 = kb - LAG
                        c.mm(psO[:, :], Vt[:, k2, h * 128:(h + 1) * 128], pts[k2][:],
                             start=(k2 == 0), stop=(k2 == nkb - 1))
                    if kb < nkb:
                        c.mm(pss[:, TB:2 * TB], KT[64:128, h, kb * 128:(kb + 1) * 128], qT[64:128, h, :],
                             serialize=(kb < LAG))
                        pt = ptile[kb % 4]
                        c.act(pt[:], pss[:, :], AF.Exp, scale=0.125)
                        dk = kb - (nkb - NT)
                        if dk >= 0:
                            c.tt(pt.v(pt.t[:, :].rearrange("p (a b) -> p a b", a=2)),
                                 pt.v(pt.t[:, :].rearrange("p (a b) -> p a b", a=2)),
                                 maskb.v(maskb.t[:, dk:dk + 1, :].to_broadcast([128, 2, TB])), ALU.mult)
                        if kb == 0:
                            c.copy(Pacc[:, :], pt[:])
                        else:
                            c.tt(Pacc[:, :], Pacc[:, :], pt[:], ALU.add)
                        pts[kb] = pt
                    ssd_step()
                    if kb == LAG and pend is not None:
                        attn_finalize(*pend)
                        pend = None
                pend = (h, psO, Pacc)
            attn_finalize(*pend)
            for _ in sgen:
                pass

            ck(6)
            ck(7)
            for oc in range(0, 8, 4):
                for (wA, wG, cG, yT, first) in (("so", "in", C_GS, ysT, True), ("ao", "in", C_GA, yaT, False)):
                    s_a = load_slab(wA, 0, 8, oc * 128, 512)
                    s_g = load_slab(wG, 0, 8, cG + oc * 128, 512)

                    def g0(j, stt_, s_a=s_a, s_g=s_g, yT=yT):
                        w = slice(j * 128, (j + 1) * 128)
                        psA = c.psum()
                        for kc in range(8):
                            c.mm(psA[:, 0:TB], s_a[:, kc, w], yT[:, kc, :], start=(kc == 0), stop=(kc == 7))
                        psC = c.psum()
                        for kc in range(8):
                            c.mm(psC[:, 0:TB], s_g[:, kc, w], hT[:, kc, HX:HX + TB], start=(kc == 0), stop=(kc == 7))
                        stt_["A"] = psA
                        stt_["C"] = psC

                    def g1(j, stt_, first=first, oc=oc):
                        sg0 = TF()[:, 0:TB]
                        c.act(sg0, stt_["C"][:, 0:TB], AF.Sigmoid)
                        if first:
                            c.tt(m1b[j][:], stt_["A"][:, 0:TB], sg0, ALU.mult)
                        else:
                            mg2 = TF()[:, 0:TB]
                            c.tt(mg2, stt_["A"][:, 0:TB], sg0, ALU.mult)
                            c.tt(mgT[:, oc + j, :], m1b[j][:], mg2, ALU.add, eng="pool")

                    pipeline(4, [g0, g1])

            ck(8)
            def tm_norm_residual(wname, lhs_fn, nk, Gw):
                psF = [[psacc() for i in range(NT)] for hf in range(2)]
                for hf in range(2):
                    for k0 in range(0, nk, 8):
                        n = min(8, nk - k0)
                        s = load_slab(wname, k0, n, hf * 512, 512)
                        for i in range(NT):
                            for kk in range(n):
                                c.mm(psF[hf][i][:, :], lhs_fn(k0 + kk, i), s[:, kk, :],
                                     start=(k0 + kk == 0), stop=(k0 + kk == nk - 1))
                for i in range(NT):
                    c.act(junk[:, 0:512], psF[0][i][:, :], AF.Square, accum=ssq[:, 12:13])
                    c.act(junk[:, 512:1024], psF[1][i][:, :], AF.Square, accum=ssq[:, 13:14])
                    c.tt(ssq[:, 14:15], ssq[:, 12:13], ssq[:, 13:14], ALU.add)
                    rstd_from_ss(ssq[:, 15:16], ssq[:, 14:15], D, 1e-6)
                    for hf in range(2):
                        hs = slice(hf * 512, (hf + 1) * 512)
                        c.stt(ytmp[:, hs], psF[hf][i][:, :], ssq[:, 15:16], Gw[:, hs], ALU.mult, ALU.mult)
                    c.tt(xres[:, i, :], xres[:, i, :], ytmp[:], ALU.add)

            tm_norm_residual("out", lambda k, i: mgT[:, k, i * 128:(i + 1) * 128], 8, G1W)

            ck(9)
            norm_to_T(xres, A2, 16, b, h2halo, 2)
            fslab = {}

            def f0(jj, stt_):
                if jj % 4 == 0:
                    n = min(4, NJ - jj)
                    fslab["g"] = load_slab("up", 0, 8, jj * 128, n * 128)
                    fslab["v"] = load_slab("up", 0, 8, DFF + jj * 128, n * 128)
                w = slice((jj % 4) * 128, (jj % 4 + 1) * 128)
                psg = c.psum()
                for kc in range(8):
                    c.mm(psg[:, 0:2 + TB], fslab["g"][:, kc, w], hT[:, kc, HX - 2:HX + TB], start=(kc == 0), stop=(kc == 7))
                psv = c.psum()
                for kc in range(8):
                    c.mm(psv[:, 0:2 + TB], fslab["v"][:, kc, w], hT[:, kc, HX - 2:HX + TB], start=(kc == 0), stop=(kc == 7))
                stt_["g"] = {"ps": psg}
                stt_["v"] = {"ps": psv}

            cg = conv_stages(3, None, "cwf", "cbf", lambda jj: jj, None)[0]
            cv = conv_stages(3, None, "cwf", "cbf", lambda jj: NJ + jj, None)[0]

            def f1(jj, stt_):
                cg(jj, stt_["g"])
                cv(jj, stt_["v"])

            def f2(jj, stt_):
                gl = TF()[:, 0:TB]
                c.act(gl, stt_["g"]["acc"], AF.Gelu_apprx_tanh)
                c.tt(mT[:, jj, :], gl, stt_["v"]["acc"], ALU.mult, eng="pool")

            pipeline(NJ, [f0, f1, f2])

            ck(10)
            tm_norm_residual("dn", lambda k, i: mT[:, k, i * 128:(i + 1) * 128], NJ, G2W)
            c.dma("pool", y_d[b, t0:t0 + TB, :].rearrange("(i p) f -> p i f", p=128), xres[:])

    c.finish("pool", xb)
    c.finish("sp", xb)
    return nc, c


_CACHE = {}


def _get_nc(nseq, nblk):
    key = (nseq, nblk)
    if key not in _CACHE:
        _CACHE[key] = build_nc(nseq, nblk)[0]
    return _CACHE[key]


def kernel(**inputs):
    inp = {k: np.asarray(v) for k, v in inputs.items()}
    x = inp["x"].astype(np.float32, copy=False)
    B = x.shape[0]
    nseq = B // NCORES
    nblk = x.shape[1] // TB
    P, R = _host_params(inp)
    C = _host_consts()
    shared = {
        "w_ada": np.ascontiguousarray(inp["w_ada"][0], dtype=np.float32),
        "w_in": np.ascontiguousarray(inp["w_in"][0], dtype=np.float32),
        "w_ssd_o": np.ascontiguousarray(inp["w_ssd_o"][0], dtype=np.float32),
        "w_attn_o": np.ascontiguousarray(inp["w_attn_o"][0], dtype=np.float32),
        "w_out": np.ascontiguousarray(inp["w_out"][0], dtype=np.float32),
        "w_up": np.ascontiguousarray(inp["w_up"][0], dtype=np.float32),
        "w_down": np.ascontiguousarray(inp["w_down"][0], dtype=np.float32),
        "params": P, "rows": R, "consts": C,
    }
    in_maps = []
    for ci in range(NCORES):
        cs = inp["c"][ci * nseq:(ci + 1) * nseq].astype(np.float32)
        cT = np.ascontiguousarray(cs.reshape(nseq, 8, 128).transpose(2, 1, 0))
        m = dict(shared)
        m["x"] = np.ascontiguousarray(x[ci * nseq:(ci + 1) * nseq])
        m["cT"] = cT
        in_maps.append(m)
    nc = _get_nc(nseq, nblk)
    res = run_bass_kernel_spmd(nc, in_maps, core_ids=list(range(NCORES)))
    out = np.concatenate([np.asarray(r["y"]) for r in res.results], axis=0)
    return out.astype(np.float32, copy=False)
```

```python
import math
import numpy as np
import concourse.bass as bass
import concourse.mybir as mybir
from concourse.bass_utils import run_bass_kernel_spmd

F32 = mybir.dt.float32
BF16 = mybir.dt.bfloat16
I32 = mybir.dt.int32
ALU = mybir.AluOpType
AF = mybir.ActivationFunctionType
AX = mybir.AxisListType

NCORES = 8
D = 1024
SEQ = 2048
TB = 256
NT = TB // 128
DFF = 2816
NJ = DFF // 128
INW = 7696
C_Z, C_X, C_B, C_C, C_DT, C_Q, C_K, C_V, C_GS, C_GA = 0, 1024, 2048, 2304, 2560, 2576, 3600, 4624, 5648, 6672
ROPE_THETA = 500000.0
PI = math.pi


class View:
    __slots__ = ("buf", "ap")

    def __init__(self, buf, ap):
        self.buf = buf
        self.ap = ap


class Buf:
    def __init__(self, ctx, name, shape, dtype, space="sbuf"):
        self.ctx = ctx
        self.name = name
        self.space = space
        ctx.allbufs.append(self)
        nc = ctx.nc
        if space == "sbuf":
            self.t = nc.alloc_sbuf_tensor(name, list(shape), dtype)
        elif space == "psum":
            self.t = nc.alloc_psum_tensor(name, list(shape), dtype)
        else:
            self.t = nc.dram_tensor(name, list(shape), dtype)
        self.wr = {}
        self.rd = {}
        self.dkey = None
        self.dcnt = 0
        self.aliases = []

    def __getitem__(self, idx):
        return View(self, self.t[idx])

    def v(self, ap):
        return View(self, ap)


class VBuf(Buf):
    def __init__(self, ctx, name, ap, aliases=()):
        self.ctx = ctx
        self.name = name
        self.space = "sbuf"
        ctx.allbufs.append(self)
        self.t = ap
        self.wr = {}
        self.rd = {}
        self.dkey = None
        self.dcnt = 0
        self.aliases = list(aliases)


def _mx(d, k, v):
    if d.get(k, 0) < v:
        d[k] = v


class Ctx:
    ENG = ("pe", "act", "dve", "pool", "sp")

    def __init__(self, nc):
        self.nc = nc
        self.e = {"pe": nc.tensor, "act": nc.scalar, "dve": nc.vector,
                  "pool": nc.gpsimd, "sp": nc.sync}
        self.sems = {}
        self.ekey = {}
        self.ecnt = {}
        self.seen = {k: {} for k in self.ENG}
        for k in self.ENG:
            key = "e_" + k
            self.sems[key] = nc.alloc_semaphore(key)
            self.ekey[k] = key
            self.ecnt[k] = 0
        self.ninstr = 0
        self.nwaits = 0
        self.allbufs = []
        self.nop = {}
        self.marks = []
        self.dcnt = {}
        self._ps = []
        self._psi = 0

    def buf(self, name, shape, dtype, space="sbuf"):
        return Buf(self, name, shape, dtype, space)

    def _wait(self, eng, need):
        E = self.e[eng]
        seen = self.seen[eng]
        for k, v in need.items():
            if seen.get(k, 0) < v:
                E.wait_ge(self.sems[k], v)
                seen[k] = v
                self.nwaits += 1

    def op(self, eng, emit, reads=(), writes=(), signal=True):
        need = {}
        own = self.ekey[eng]
        rb = [r.buf if isinstance(r, View) else r for r in reads]
        wb = [w.buf if isinstance(w, View) else w for w in writes]
        for b in rb:
            for k, v in b.wr.items():
                _mx(need, k, v)
            if b.space == "psum":
                for k, v in b.rd.items():
                    if k != own:
                        _mx(need, k, v)
        for b0 in wb:
            for b in [b0] + b0.aliases:
                for k, v in b.wr.items():
                    if k != own or eng != "pe":
                        _mx(need, k, v)
                for k, v in b.rd.items():
                    if k != own or eng != "pe":
                        _mx(need, k, v)
        for b0 in rb:
            for b in b0.aliases:
                for k, v in b.wr.items():
                    _mx(need, k, v)
        self._wait(eng, need)
        ins = emit(self.e[eng])
        self.ninstr += 1
        self.nop[eng] = self.nop.get(eng, 0) + 1
        if signal:
            self.ecnt[eng] += 1
            ins.then_inc(self.sems[own], 1)
            t = self.ecnt[eng]
        else:
            t = self.ecnt[eng] + 1
        for b in rb:
            _mx(b.rd, own, t)
        for b in wb:
            _mx(b.wr, own, t)
        return ins

    def dma(self, eng, out, in_, **kw):
        need = {}
        tracked = []
        if isinstance(out, View):
            b = out.buf
            for d in (b.wr, b.rd):
                for k, v in d.items():
                    _mx(need, k, v)
            tracked.append((b, "w"))
            oap = out.ap
        else:
            oap = out
        if isinstance(in_, View):
            b = in_.buf
            for k, v in b.wr.items():
                _mx(need, k, v)
            tracked.append((b, "r"))
            iap = in_.ap
        else:
            iap = in_
        self._wait(eng, need)
        b0 = tracked[0][0]
        dkey = "d_%s_%s" % (eng, b0.name)
        if dkey not in self.sems:
            self.sems[dkey] = self.nc.alloc_semaphore(dkey)
            self.dcnt[dkey] = 0
        self.dcnt[dkey] += 16
        ins = self.e[eng].dma_start(out=oap, in_=iap, **kw)
        ins.then_inc(self.sems[dkey], 16)
        self.ninstr += 1
        for b, m in tracked:
            if m == "w":
                _mx(b.wr, dkey, self.dcnt[dkey])
            else:
                _mx(b.rd, dkey, self.dcnt[dkey])
        return ins

    def finish(self, eng, bufs):
        need = {}
        for b in bufs:
            for d in (b.wr, b.rd):
                for k, v in d.items():
                    _mx(need, k, v)
        self._wait(eng, need)

    def psum(self, hold=False):
        n = len(self._ps)
        for _ in range(n):
            b = self._ps[self._psi % n]
            self._psi += 1
            if not getattr(b, "held", False):
                break
        else:
            raise RuntimeError("all PSUM banks held")
        b.held = hold
        return b

    def release(self, b):
        b.held = False

    def mm(self, out, lhsT, rhs, start=True, stop=True, signal=None, serialize=False):
        if signal is None:
            signal = stop
        if serialize:
            own = self.ekey["pe"]
            v = out.buf.wr.get(own, 0)
            if v and self.seen["pe"].get(own, 0) < v:
                self.e["pe"].wait_ge(self.sems[own], v)
                self.seen["pe"][own] = v
        return self.op("pe", lambda E: E.matmul(out.ap, lhsT.ap, rhs.ap, start=start, stop=stop),
                       [lhsT, rhs], [out], signal=signal)

    def tr(self, out, in_, ident, signal=True):
        return self.op("pe", lambda E: E.transpose(out.ap, in_.ap, ident.ap), [in_, ident], [out],
                       signal=signal)

    def act(self, out, in_, func, scale=1.0, bias=None, accum=None, eng="act"):
        reads = [in_]
        writes = [out]
        kw = {}
        if isinstance(scale, View):
            reads.append(scale)
            kw["scale"] = scale.ap
        else:
            kw["scale"] = float(scale)
        if bias is not None:
            if isinstance(bias, View):
                reads.append(bias)
                kw["bias"] = bias.ap
            else:
                kw["bias"] = float(bias)
        if accum is not None:
            writes.append(accum)
            kw["accum_out"] = accum.ap
        return self.op(eng, lambda E: E.activation(out.ap, in_.ap, func, **kw), reads, writes)

    def tt(self, out, in0, in1, op, eng="dve"):
        return self.op(eng, lambda E: E.tensor_tensor(out.ap, in0.ap, in1.ap, op), [in0, in1], [out])

    def ts(self, out, in0, s1, s2=None, op0=ALU.mult, op1=None, eng="dve"):
        reads = [in0]
        a1 = s1
        a2 = s2
        if isinstance(s1, View):
            reads.append(s1)
            a1 = s1.ap
        if isinstance(s2, View):
            reads.append(s2)
            a2 = s2.ap
        if op1 is None:
            return self.op(eng, lambda E: E.tensor_scalar(out.ap, in0.ap, a1, a2, op0), reads, [out])
        return self.op(eng, lambda E: E.tensor_scalar(out.ap, in0.ap, a1, a2, op0, op1), reads, [out])

    def stt(self, out, in0, s, in1, op0, op1, eng="dve"):
        reads = [in0, in1]
        a = s
        if isinstance(s, View):
            reads.append(s)
            a = s.ap
        return self.op(eng, lambda E: E.scalar_tensor_tensor(out.ap, in0.ap, a, in1.ap, op0, op1),
                       reads, [out])

    def copy(self, out, in_, eng="dve"):
        return self.op(eng, lambda E: E.tensor_copy(out.ap, in_.ap), [in_], [out])

    def memset(self, out, val, eng="dve"):
        return self.op(eng, lambda E: E.memset(out.ap, val), [], [out])

    def recip(self, out, in_):
        return self.op("dve", lambda E: E.reciprocal(out.ap, in_.ap), [in_], [out])


def _param_layout():
    items = [("pn1", 8), ("pn2", 8), ("cws", 48), ("cbs", 12), ("cwf", 132), ("cbf", 44),
             ("subw", 1), ("bada", 48), ("invf", 1), ("dtb", 16), ("alog", 16), ("dsk", 16),
             ("lam", 256), ("snw", 1024)]
    off = {}
    o = 0
    for n, w in items:
        off[n] = (o, w)
        o += w
    return off, o


PO, NP = _param_layout()
CO = {"ident": (0, 128), "U": (128, 128), "tri": (256, 128), "sigT": (384, 128), "iota": (512, 256),
      "mask0": (768, 256), "mask1": (1024, 256)}
NCST = 1280
RO = {"bada": (0, 6144), "post1": (6144, 1024), "post2": (7168, 1024)}
NROW = 8192


def _col(v, n):
    return np.ascontiguousarray(np.asarray(v, np.float32).reshape(n, 128).T)


def _rep(v):
    v = np.asarray(v, np.float32).reshape(1, -1)
    return np.repeat(v, 128, axis=0)


def _host_params(inp):
    P = np.zeros((128, NP), np.float32)

    def put(name, arr):
        o, w = PO[name]
        P[:, o:o + w] = np.asarray(arr, np.float32).reshape(128, w)

    put("pn1", _col(inp["pre_norm1_w"][0], 8))
    put("pn2", _col(inp["pre_norm2_w"][0], 8))
    put("cws", np.asarray(inp["conv_ssd_w"][0]).reshape(4, 12, 128).transpose(2, 1, 0))
    put("cbs", _col(inp["conv_ssd_b"][0], 12))
    put("cwf", np.asarray(inp["conv_ffn_w"][0]).reshape(3, 44, 128).transpose(2, 1, 0))
    put("cbf", _col(inp["conv_ffn_b"][0], 44))
    put("subw", np.asarray(inp["subln_w"][0]).reshape(128, 1))
    put("bada", _col(inp["b_ada"][0], 48))
    invf = np.zeros((128, 1), np.float32)
    for p in range(128):
        d = p % 64
        if d < 16:
            invf[p, 0] = np.float32(np.power(np.float32(ROPE_THETA), np.float32(-(d % 8) / 8.0)))
    put("invf", invf)
    put("dtb", _rep(inp["dt_bias"][0]))
    put("alog", _rep(inp["a_log"][0]))
    put("dsk", _rep(inp["d_skip"][0]))
    put("lam", _rep(np.concatenate([inp["lambda_q1"][0], inp["lambda_k1"][0],
                                    inp["lambda_q2"][0], inp["lambda_k2"][0]])))
    put("snw", _rep(inp["ssd_norm_w"][0]))
    R = np.zeros((1, NROW), np.float32)
    R[0, 0:6144] = inp["b_ada"][0]
    R[0, 6144:7168] = inp["post_norm1_w"][0]
    R[0, 7168:8192] = inp["post_norm2_w"][0]
    return P, R


def _host_consts():
    C = np.zeros((128, NCST), np.float32)
    i = np.arange(128)
    C[:, 0:128] = np.eye(128, dtype=np.float32)
    C[:, 128:256] = (i[:, None] > i[None, :]).astype(np.float32)
    tri = (i[:, None] <= i[None, :]).astype(np.float32)
    C[:, 256:384] = tri
    sig = np.zeros((128, 128), np.float32)
    for cb in (0, 64):
        for d in range(8):
            sig[cb + d + 8, cb + d] = -1.0
            sig[cb + d, cb + d + 8] = 1.0
    C[:, 384:512] = sig
    C[:, 512:768] = np.arange(256, dtype=np.float32)[None, :]
    C[:, 768:896] = tri
    C[:, 896:1024] = 1.0
    C[:, 1024:1152] = 0.0
    C[:, 1152:1280] = tri
    return C


class _Stop(Exception):
    pass


STOP = [0]


def build_nc(NSEQ=4, NBLK=8):
    try:
        return _build_nc(NSEQ, NBLK)
    except _Stop as e:
        nc, c = e.args
        for eng in ("sp", "pool", "act", "dve", "pe"):
            c.finish(eng, c.allbufs)
        return nc, c


def _build_nc(NSEQ=4, NBLK=8):
    nc = bass.Bass("TRN2", target_bir_lowering=False)
    T = NBLK * TB
    x_d = nc.dram_tensor("x", [NSEQ, T, D], F32, kind="ExternalInput").ap()
    cT_d = nc.dram_tensor("cT", [128, 8, NSEQ], F32, kind="ExternalInput").ap()
    wada_d = nc.dram_tensor("w_ada", [D, 6 * D], F32, kind="ExternalInput").ap()
    win_d = nc.dram_tensor("w_in", [D, INW], F32, kind="ExternalInput").ap()
    wso_d = nc.dram_tensor("w_ssd_o", [D, D], F32, kind="ExternalInput").ap()
    wao_d = nc.dram_tensor("w_attn_o", [D, D], F32, kind="ExternalInput").ap()
    wout_d = nc.dram_tensor("w_out", [D, D], F32, kind="ExternalInput").ap()
    wup_d = nc.dram_tensor("w_up", [D, 2 * DFF], F32, kind="ExternalInput").ap()
    wdn_d = nc.dram_tensor("w_down", [DFF, D], F32, kind="ExternalInput").ap()
    par_d = nc.dram_tensor("params", [128, NP], F32, kind="ExternalInput").ap()
    row_d = nc.dram_tensor("rows", [1, NROW], F32, kind="ExternalInput").ap()
    cst_d = nc.dram_tensor("consts", [128, NCST], F32, kind="ExternalInput").ap()
    y_d = nc.dram_tensor("y", [NSEQ, T, D], F32, kind="ExternalOutput").ap()

    c = Ctx(nc)

    def ck(k):
        c.marks.append((k, dict(c.nop)))
        if STOP[0] == k:
            raise _Stop(nc, c)

    c._ps = [c.buf("ps%d" % i, [128, 512], F32, "psum") for i in range(7)]

    def psacc():
        return c.psum()

    tpb = [c.buf("tp%d" % i, [128, 8, 128], BF16, "psum") for i in range(1)]
    tpi = [0]

    def tpbank():
        b = tpb[tpi[0] % len(tpb)]
        tpi[0] += 1
        return b

    cst = c.buf("cst", [128, 512], F32)
    par = c.buf("par", [128, NP], F32)
    c.dma("sp", cst[:, 0:256], cst_d[:, 128:384])
    c.dma("sp", cst[:, 256:512], cst_d[:, 512:768])
    c.dma("sp", par[:], par_d)

    def P(name, a=None, b=None):
        o, w = PO[name]
        lo = o if a is None else o + a
        hi = o + w if b is None else o + b
        return par[:, lo:hi]

    def CS(name):
        o, w = {"U": (0, 128), "tri": (128, 128), "iota": (256, 256)}[name]
        return cst[:, o:o + w]

    identb = c.buf("identb", [128, 128], BF16)
    trib = c.buf("trib", [128, 128], BF16)
    sigTb = c.buf("sigTb", [128, 128], BF16)
    onesb = c.buf("onesb", [128, 128], BF16)
    onesf = c.buf("onesf", [128, 128], F32)
    maskb = c.buf("maskb", [128, 2, 256], BF16)
    c.dma("pool", identb[:], cst_d[:, 0:128])
    c.dma("pool", trib[:], cst_d[:, 256:384])
    c.dma("pool", sigTb[:], cst_d[:, 384:512])
    c.memset(onesb[:], 1.0)
    c.memset(onesf[:], 1.0)
    c.dma("pool", maskb[:, 0, :], cst_d[:, 768:1024])
    c.dma("pool", maskb[:, 1, :], cst_d[:, 1024:1280])
    Uf = CS("U")
    trif = CS("tri")
    iota = CS("iota")

    Wb = {}
    for name, ap, KC, N in (("in", win_d, 8, INW), ("so", wso_d, 8, D), ("ao", wao_d, 8, D),
                            ("out", wout_d, 8, D), ("up", wup_d, 8, 2 * DFF), ("dn", wdn_d, NJ, D)):
        Wb[name] = c.buf("wb_" + name, [128, KC, N], BF16, "dram")
        src = ap.rearrange("(kc p) n -> p kc n", p=128)
        for kc in range(KC):
            c.dma("pool", Wb[name][:, kc, :], src[:, kc, :])
    wada_src = wada_d.rearrange("(kc p) n -> p kc n", p=128)

    NSLAB = 4
    slabs = [c.buf("slab%d" % i, [128, 8, 512], BF16) for i in range(NSLAB)]
    sli = [0]

    def load_slab(name, k0, nk, col0, ncols):
        s = slabs[sli[0] % NSLAB]
        sli[0] += 1
        c.dma("sp", s[:, 0:nk, 0:ncols], Wb[name][:, k0:k0 + nk, col0:col0 + ncols])
        return s

    def load_slab_ada(col0, ncols):
        s = slabs[sli[0] % NSLAB]
        sli[0] += 1
        c.dma("pool", s[:, 0:8, 0:ncols], wada_src[:, :, col0:col0 + ncols])
        return s

    Wdt = c.buf("Wdt", [128, 8, 16], BF16)
    c.dma("sp", Wdt[:], Wb["in"][:, :, C_DT:C_DT + 16])

    sm = c.buf("sm", [128, 64], F32)
    lamt = c.buf("lamt", [128, 128], F32)
    c.tt(lamt[:, 0:64], P("lam", 0, 64), P("lam", 64, 128), ALU.mult)
    c.tt(lamt[:, 64:128], P("lam", 128, 192), P("lam", 192, 256), ALU.mult)
    c.op("dve", lambda E: E.reduce_sum(sm[:, 0:1].ap, lamt[:, 0:64].ap, axis=AX.X), [lamt], [sm])
    c.op("dve", lambda E: E.reduce_sum(sm[:, 1:2].ap, lamt[:, 64:128].ap, axis=AX.X), [lamt], [sm])
    c.act(sm[:, 2:4], sm[:, 0:2], AF.Exp)
    c.tt(sm[:, 4:5], sm[:, 3:4], sm[:, 2:3], ALU.subtract)
    c.ts(sm[:, 5:6], sm[:, 4:5], -0.2, None, ALU.add)
    nlam = sm[:, 5:6]
    c.ts(sm[:, 6:7], P("subw"), 0.8, None, ALU.mult)
    subw = sm[:, 6:7]
    arow = c.buf("arow", [128, 16], F32)
    c.act(arow[:], P("alog"), AF.Exp)
    c.ts(arow[:], arow[:], -1.0, None, ALU.mult)

    cTs = c.buf("cTs", [128, 8, NSEQ], F32)
    c.dma("sp", cTs[:], cT_d)
    scT = c.buf("scT", [128, 8, NSEQ], F32)
    c.act(scT[:], cTs[:], AF.Silu)
    scTb = c.buf("scTb", [128, 8, NSEQ], BF16)
    c.copy(scTb[:], scT[:])
    modT = c.buf("modT", [128, 32, NSEQ], F32)
    mod_cols = [0, 1024, 3072, 4096]
    psm = c.psum()
    for gi, col0 in enumerate(mod_cols):
        for half in range(2):
            s = load_slab_ada(col0 + half * 512, 512)
            for cc in range(4):
                ci = gi * 8 + half * 4 + cc
                for kc in range(8):
                    c.mm(psm[:, ci * NSEQ:(ci + 1) * NSEQ], s[:, kc, cc * 128:(cc + 1) * 128],
                         scTb[:, kc, :], start=(kc == 0), stop=(kc == 7))
    bo = PO["bada"][0]
    for gi, col0 in enumerate(mod_cols):
        fc = col0 // 128
        c.tt(modT[:, gi * 8:(gi + 1) * 8, :],
             psm.v(psm.t[:, gi * 8 * NSEQ:(gi + 1) * 8 * NSEQ].rearrange("p (a b) -> p a b", b=NSEQ)),
             par.v(par.t[:, bo + fc:bo + fc + 8].unsqueeze(2).to_broadcast([128, 8, NSEQ])), ALU.add)
    A1 = c.buf("A1", [128, 8, NSEQ], F32)
    A2 = c.buf("A2", [128, 8, NSEQ], F32)
    for Ab, pn, sc0 in ((A1, "pn1", 8), (A2, "pn2", 24)):
        o, w = PO[pn]
        c.ts(Ab[:], modT[:, sc0:sc0 + 8, :], 1.0, None, ALU.add)
        c.tt(Ab[:], Ab[:], par.v(par.t[:, o:o + 8].unsqueeze(2).to_broadcast([128, 8, NSEQ])), ALU.mult)

    ck(1)
    KT = c.buf("KT", [128, 8, T], BF16)
    Vt = c.buf("Vt", [128, T // 128, D], BF16)
    xb = [c.buf("xres0", [128, NT, D], F32)]
    HX = 3
    hT = c.buf("hT", [128, 8, HX + TB], BF16)
    h1halo = c.buf("h1halo", [128, 8, 3], BF16)
    h2halo = c.buf("h2halo", [128, 8, 2], BF16)
    qT = c.buf("qT", [128, 8, TB], BF16)
    yaT = c.buf("yaT", [128, 8, TB], BF16)
    ysT = c.buf("ysT", [128, 8, TB], BF16)
    mgT = qT
    xact = c.buf("xact", [128, 12, TB], BF16)
    zs = c.buf("zs", [128, NT, D], BF16)
    G1W = c.buf("G1W", [128, D], F32)
    G2W = c.buf("G2W", [128, D], F32)
    Sst = c.buf("Sst", [128, D], F32)
    Sbf = c.buf("Sbf", [128, D], BF16)
    cosb = c.buf("cosb", [128, TB], F32)
    sinb = c.buf("sinb", [128, TB], F32)
    scr = c.buf("scr", [128, 4096], F32)
    scr_bf = scr.t[:, :].bitcast(BF16)
    Rr = VBuf(c, "Rr", scr.t[:, 0:2048].rearrange("p (h l) -> p h l", h=16))
    Ee = VBuf(c, "Ee", scr_bf[:, 4096:6144].rearrange("p (h l) -> p h l", h=16))
    Mm = Ee
    ybuf = VBuf(c, "ybuf", scr.t[:, 3072:4096])
    mT = VBuf(c, "mT", scr_bf[:, 0:NJ * TB].rearrange("p (j t) -> p j t", j=NJ), aliases=[Rr, Ee])
    m1all = c.buf("m1all", [128, 4 * TB], F32)
    m1b = [VBuf(c, "m1b%d" % i, m1all.t[:, i * TB:(i + 1) * TB]) for i in range(4)]
    PaccV = [VBuf(c, "pacc%d" % i, m1all.t[:, 2 * i * TB:(2 * i + 2) * TB],
                  aliases=[m1b[2 * i], m1b[2 * i + 1]]) for i in range(2)]
    for i in range(4):
        m1b[i].aliases = [PaccV[i // 2]]
    Pb = c.buf("Pb", [128, 2 * TB], BF16)
    Rr.aliases = [mT]
    Ee.aliases = [mT]
    ytmp = c.buf("ytmp", [128, D], F32)
    junk = VBuf(c, "junk", ytmp.t[:, 0:512].bitcast(BF16), aliases=[ytmp])
    ytmp.aliases = [junk]
    xn = c.buf("xn", [128, D], BF16)
    ytok = xn
    ssq = c.buf("ssq", [128, 16], F32)
    qraw = [c.buf("qraw%d" % i, [128, TB], BF16) for i in range(2)]
    ptile = [c.buf("pt%d" % i, [128, 2 * TB], BF16) for i in range(4)]
    tfp = [c.buf("tf%d" % i, [128, TB + 4], F32) for i in range(8)]
    tfi = [0]

    def TF():
        bb = tfp[tfi[0] % len(tfp)]
        tfi[0] += 1
        return bb

    dtv = c.buf("dtv", [128, NT, 16], F32)
    lav = c.buf("lav", [128, NT, 16], F32)
    dtt = c.buf("dtt", [128, 16], F32)
    xs_tok = c.buf("xs_tok", [128, D], BF16)
    B_tok = c.buf("B_tok", [128, 2, 128], BF16)
    xd = c.buf("xd", [128, 16, 64], BF16)
    xdd = c.buf("xdd", [128, 16, 64], BF16)
    Gm = c.buf("Gm", [128, 2, 128], BF16)
    sml = c.buf("sml", [128, 80], F32)
    onesrow = onesf[0:1, :]

    def rstd_from_ss(dst, ss, n, eps):
        c.act(dst, ss, AF.Ln, scale=1.0 / n, bias=eps)
        c.act(dst, dst, AF.Exp, scale=-0.5)

    def norm_to_T(xsrc, Ab, boff, b, hbuf, H):
        for i in range(NT):
            c.act(junk[:], xsrc[:, i, :], AF.Square, accum=ssq[:, i:i + 1])
            rstd_from_ss(ssq[:, 4 + i:5 + i], ssq[:, i:i + 1], D, 1e-6)
            c.ts(xn[:], xsrc[:, i, :], ssq[:, 4 + i:5 + i], None, ALU.mult)
            tp = tpbank()
            for kc in range(8):
                c.tr(tp[:, kc, :], xn[:, kc * 128:(kc + 1) * 128], identb[:], signal=(kc == 7))
            for kc in range(8):
                c.act(hT[:, kc, HX + i * 128:HX + (i + 1) * 128], tp[:, kc, :], AF.Identity,
                      scale=Ab[:, kc, b:b + 1], bias=modT[:, boff + kc, b:b + 1])
        c.copy(hT[:, :, HX - H:HX], hbuf[:], eng="pool")
        c.copy(hbuf[:], hT[:, :, HX + TB - H:HX + TB], eng="pool")

    def pipeline(n, stages):
        ns = len(stages)
        st = [dict() for _ in range(n)]
        for t in range(n + ns - 1):
            for si in range(ns):
                i = t - si
                if 0 <= i < n:
                    stages[si](i, st[i])

    def proj_fm(wname, col0, nchunks, rhsT, stages, pc=0):
        slab = {}

        def s0(ci, stt_):
            if ci % 4 == 0:
                n = min(4, nchunks - ci)
                slab["s"] = load_slab(wname, 0, 8, col0 + ci * 128, n * 128)
            s = slab["s"]
            ps = c.psum()
            w = slice((ci % 4) * 128, (ci % 4 + 1) * 128)
            for kc in range(8):
                c.mm(ps[:, 0:TB + pc], s[:, kc, w], rhsT[:, kc, HX - pc:HX + TB], start=(kc == 0), stop=(kc == 7))
            stt_["ps"] = ps

        pipeline(nchunks, [s0] + list(stages))

    def proj_tm(wname, col0, nslab, lhsT_buf, consume):
        slab = {}

        def s0(it, stt_):
            hf, i = divmod(it, NT)
            if i == 0:
                slab["s"] = load_slab(wname, 0, 8, col0 + hf * 512, 512)
            s = slab["s"]
            ps = c.psum()
            for kc in range(8):
                c.mm(ps[:, :], lhsT_buf[:, kc, HX + i * 128:HX + (i + 1) * 128], s[:, kc, :],
                     start=(kc == 0), stop=(kc == 7))
            stt_["ps"] = ps

        def s1(it, stt_):
            hf, i = divmod(it, NT)
            consume(hf, i, stt_["ps"])

        pipeline(nslab * NT, [s0, s1])

    def range_reduce_sin(dst, shift, ang, angk, angf, angm):
        src = ang
        if shift != 0.0:
            sB = TF()
            src = sB.v(sB.t[:, 0:TB])
            c.ts(src, ang, shift, None, ALU.add)
        c.ts(angk, src, 1.0 / (2 * PI), None, ALU.mult)
        c.copy(angm, angk)
        c.stt(angm, angm, -2 * PI, src, ALU.mult, ALU.add)
        c.ts(angf, angm, PI, 2 * PI, ALU.is_gt, ALU.mult)
        c.tt(angm, angm, angf, ALU.subtract)
        c.ts(angf, angm, -PI, 2 * PI, ALU.is_lt, ALU.mult)
        c.tt(angm, angm, angf, ALU.add)
        c.act(dst[:], angm, AF.Sin)

    for b in range(NSEQ):
        c.memset(Sst[:], 0.0)
        c.memset(Sbf[:], 0.0)
        c.memset(h1halo[:], 0.0)
        c.memset(h2halo[:], 0.0)
        screp = xn.v(xn.t[:, :].rearrange("p (a b) -> p a b", a=8))
        c.copy(screp, scT.v(scT.t[:, :, b:b + 1].to_broadcast([128, 8, 128])))
        for Gw, gcol, prow in ((G1W, 2048, "post1"), (G2W, 5120, "post2")):
            for half in range(2):
                s = load_slab_ada(gcol + half * 512, 512)
                rb = ytmp[0:1, 0:512]
                ro = RO["bada"][0] + gcol + half * 512
                c.dma("sp", rb, row_d[:, ro:ro + 512])
                psA = c.psum()
                for kc in range(8):
                    c.mm(psA[:, :], xn.v(xn.t[:, kc * 128:(kc + 1) * 128]), s[:, kc, :], start=(kc == 0), stop=False)
                c.mm(psA[:, :], onesrow, rb, start=False, stop=True)
                rb2 = ytmp[0:1, 512:1024]
                ro2 = RO[prow][0] + half * 512
                c.dma("sp", rb2, row_d[:, ro2:ro2 + 512])
                psB = c.psum()
                c.mm(psB[:, :], onesrow, rb2, start=True, stop=True)
                gtmp = ybuf[:, 0:512]
                c.act(gtmp, psB[:, :], AF.Identity)
                c.tt(Gw[:, half * 512:(half + 1) * 512], psA[:, :], gtmp, ALU.mult)

        ck(2)
        for blk in range(NBLK):
            t0 = blk * TB
            xres = xb[0]
            c.dma("pool", xres[:], x_d[b, t0:t0 + TB, :].rearrange("(i p) f -> p i f", p=128))
            angB, angkB, angfB, angmB = TF(), TF(), TF(), TF()
            ang = angB.v(angB.t[:, 0:TB])
            angk = angkB.v(angkB.t[:, 0:TB].bitcast(I32))
            angf = angfB.v(angfB.t[:, 0:TB])
            angm = angmB.v(angmB.t[:, 0:TB])
            c.ts(ang, iota, float(t0), None, ALU.add)
            c.ts(ang, ang, P("invf"), None, ALU.mult)
            range_reduce_sin(sinb, 0.0, ang, angk, angf, angm)
            range_reduce_sin(cosb, PI / 2, ang, angk, angf, angm)
            norm_to_T(xres, A1, 0, b, h1halo, 3)
            ck(3)

            def rope_stages(dst_fn):
                def r1(ci, stt_):
                    ps = stt_["ps"]
                    qr = qraw[ci % 2]
                    c.act(qr[:], ps[:, 0:TB], AF.Identity)
                    ps2 = c.psum()
                    c.mm(ps2[:, 0:TB], sigTb[:], qr[:])
                    t1 = TF()
                    c.tt(t1[:, 0:TB], ps[:, 0:TB], cosb[:], ALU.mult)
                    stt_["ps2"] = ps2
                    stt_["t1"] = t1

                def r2(ci, stt_):
                    t2 = TF()
                    c.tt(t2[:, 0:TB], stt_["ps2"][:, 0:TB], sinb[:], ALU.mult)
                    c.tt(dst_fn(ci), stt_["t1"][:, 0:TB], t2[:, 0:TB], ALU.add, eng="pool")
                return [r1, r2]

            proj_fm("in", C_K, 8, hT, rope_stages(lambda ci: KT[:, ci, t0:t0 + TB]))
            proj_fm("in", C_Q, 8, hT, rope_stages(lambda ci: qT[:, ci, :]))
            proj_tm("in", C_V, 2, hT, lambda hf, i, ps: c.act(
                Vt[:, blk * NT + i, hf * 512:(hf + 1) * 512], ps[:, :], AF.Identity))

            ck(4)
            def conv_stages(K, halo, wname, bname, chan_fn, fin):
                H = K - 1

                def c1(ci, stt_):
                    ps = stt_["ps"]
                    ch = chan_fn(ci)
                    acc = TF()[:, 0:TB]
                    wo = PO[wname][0] + ch * K
                    bo2 = PO[bname][0] + ch
                    c.act(acc, ps[:, H:H + TB], AF.Identity, scale=par[:, wo + H:wo + H + 1],
                          bias=par[:, bo2:bo2 + 1])
                    for k in range(H):
                        c.stt(acc, ps[:, k:k + TB], par[:, wo + k:wo + k + 1], acc, ALU.mult, ALU.add)
                    stt_["acc"] = acc

                def c2(ci, stt_):
                    fin(ci, stt_)
                return [c1, c2]

            for i in range(NT):
                ps = c.psum()
                for kc in range(8):
                    c.mm(ps[:, 0:16], hT[:, kc, HX + i * 128:HX + (i + 1) * 128], Wdt[:, kc, :],
                         start=(kc == 0), stop=(kc == 7))
                c.tt(dtt[:], ps[:, 0:16], P("dtb"), ALU.add)
                c.act(dtt[:], dtt[:], AF.Exp)
                c.act(dtv[:, i, :], dtt[:], AF.Ln, bias=1.0)
                c.tt(lav[:, i, :], dtv[:, i, :], arow[:], ALU.mult)
            proj_fm("in", C_X, 12, hT, conv_stages(
                4, None, "cws", "cbs", lambda ci: ci,
                lambda ci, stt_: c.act(xact[:, ci, :], stt_["acc"], AF.Silu)), pc=3)
            proj_tm("in", C_Z, 2, hT, lambda hf, i, ps: c.act(
                zs[:, i, hf * 512:(hf + 1) * 512], ps[:, :], AF.Silu))

            def ssd_gen():
                for i in range(NT):
                    cols = slice(i * 128, (i + 1) * 128)
                    tp = tpbank()
                    for c8 in range(8):
                        c.tr(tp[:, c8, :], xact[:, c8, cols], identb[:], signal=(c8 == 7))
                    c.copy(xs_tok.v(xs_tok.t[:, :].rearrange("p (a b) -> p a b", a=8)), tp[:, :, :])
                    tp2 = tpbank()
                    for g in range(2):
                        c.tr(tp2[:, g, :], xact[:, 8 + g, cols], identb[:], signal=(g == 1))
                    c.copy(B_tok[:], tp2[:, 0:2, :])
                    xs3 = xs_tok.v(xs_tok.t[:, :].rearrange("p (h d) -> p h d", h=16))
                    c.tt(xd[:], xs3, dtv.v(dtv.t[:, i, :].unsqueeze(2).to_broadcast([128, 16, 64])), ALU.mult)
                    yield
                    psg = c.psum()
                    for g in range(2):
                        c.mm(psg[:, g * 128:(g + 1) * 128], xact[:, 8 + g, cols], xact[:, 10 + g, cols])
                    c.tt(Gm[:], psg.v(psg.t[:, 0:256].rearrange("p (g l) -> p g l", g=2)),
                         cst.v(trif.ap.unsqueeze(1).to_broadcast([128, 2, 128])), ALU.mult)
                    c.tt(Rr[:], lav.v(lav.t[:, i, :].unsqueeze(2).to_broadcast([128, 16, 128])),
                         cst.v(trif.ap.unsqueeze(1).to_broadcast([128, 16, 128])), ALU.mult)
                    yield
                    for q4 in range(4):
                        pse = c.psum()
                        c.mm(pse[:, :], Uf, Rr.v(Rr.t[:, q4 * 4:(q4 + 1) * 4, :].rearrange("p h l -> p (h l)")))
                        c.act(Ee.v(Ee.t[:, q4 * 4:(q4 + 1) * 4, :].rearrange("p h l -> p (h l)")), pse[:, :], AF.Exp)
                        g = q4 // 2
                        c.tt(Mm[:, q4 * 4:(q4 + 1) * 4, :], Ee[:, q4 * 4:(q4 + 1) * 4, :],
                             Gm.v(Gm.t[:, g:g + 1, :].to_broadcast([128, 4, 128])), ALU.mult)
                        yield
                    psc = c.psum()
                    c.mm(psc[:, 0:16], trif, lav[:, i, :], signal=False)
                    c.mm(psc[:, 16:32], onesf[:], lav[:, i, :])
                    c.copy(sml[:, 0:16], psc[:, 0:16])
                    c.tt(sml[:, 16:32], psc[:, 16:32], sml[:, 0:16], ALU.subtract)
                    c.act(sml[:, 32:48], sml[:, 16:32], AF.Exp)
                    c.act(sml[:, 48:64], sml[:, 0:16], AF.Exp)
                    c.act(sml[:, 64:80], psc[:, 16:32], AF.Exp)
                    c.tt(xdd[:], xd[:], sml.v(sml.t[:, 32:48].unsqueeze(2).to_broadcast([128, 16, 64])), ALU.mult)
                    yield
                    psY = [psacc(), psacc()]
                    for h in range(16):
                        c.mm(psY[h // 8][:, (h % 8) * 64:(h % 8 + 1) * 64], Mm[:, h, :], xd[:, h, :],
                             signal=(h % 8 == 7))
                    psOf = [psacc(), psacc()]
                    for g in range(2):
                        c.mm(psOf[g][:, :], xact[:, 10 + g, cols], Sbf[:, g * 512:(g + 1) * 512])
                    for g in range(2):
                        hs = slice(g * 512, (g + 1) * 512)
                        c.tt(ybuf.v(ybuf.t[:, hs].rearrange("p (h d) -> p h d", h=8)),
                             psOf[g].v(psOf[g].t[:, :].rearrange("p (h d) -> p h d", h=8)),
                             sml.v(sml.t[:, 48 + g * 8:56 + g * 8].unsqueeze(2).to_broadcast([128, 8, 64])),
                             ALU.mult)
                        c.tt(ybuf[:, hs], ybuf[:, hs], psY[g][:, :], ALU.add)
                    yield
                    psS = [c.psum(), c.psum()]
                    for g in range(2):
                        c.mm(psS[g][:, :], B_tok[:, g, :],
                             xdd.v(xdd.t[:, g * 8:(g + 1) * 8, :].rearrange("p h d -> p (h d)")))
                    for g in range(2):
                        hs = slice(g * 512, (g + 1) * 512)
                        c.tt(Sst.v(Sst.t[:, hs].rearrange("p (h d) -> p h d", h=8)),
                             Sst.v(Sst.t[:, hs].rearrange("p (h d) -> p h d", h=8)),
                             sml.v(sml.t[:, 64 + g * 8:72 + g * 8].unsqueeze(2).to_broadcast([128, 8, 64])),
                             ALU.mult)
                        c.tt(Sst[:, hs], Sst[:, hs], psS[g][:, :], ALU.add)
                    c.copy(Sbf[:], Sst[:], eng="pool")
                    yield
                    c.tt(ytmp.v(ytmp.t[:, :].rearrange("p (h d) -> p h d", h=16)), xs3,
                         par.v(P("dsk").ap.unsqueeze(2).to_broadcast([128, 16, 64])), ALU.mult)
                    c.tt(ybuf[:], ybuf[:], ytmp[:], ALU.add)
                    c.tt(ybuf[:], ybuf[:], zs[:, i, :], ALU.mult)
                    yield
                    for g in range(2):
                        hs = slice(g * 512, (g + 1) * 512)
                        c.act(junk[:, 0:512], ybuf[:, hs], AF.Square, accum=ssq[:, 8 + g:9 + g])
                        rstd_from_ss(ssq[:, 10 + g:11 + g], ssq[:, 8 + g:9 + g], 512, 1e-5)
                        so = PO["snw"][0]
                        c.stt(ytok[:, hs], ybuf[:, hs], ssq[:, 10 + g:11 + g],
                              par[:, so + g * 512:so + (g + 1) * 512], ALU.mult, ALU.mult)
                    tp = tpbank()
                    for c8 in range(8):
                        c.tr(tp[:, c8, :], ytok[:, c8 * 128:(c8 + 1) * 128], identb[:], signal=(c8 == 7))
                    c.copy(ysT[:, :, cols], tp[:, :, :])
                    yield


            ck(5)
            nkb = NT * blk + NT
            LAG = 3 if nkb >= 6 else 2

            def attn_finalize(h, psO, Pacc):
                c.copy(Pb[:], Pacc[:, :], eng="pool")
                psL = c.psum()
                c.mm(psL[:, :], onesb[:], Pb[:])
                rr, on = TF(), TF()
                r2 = ytmp[:, 0:2 * TB]
                o2 = ytmp[:, 2 * TB:4 * TB]
                c.act(r2, psL[:, :], AF.Ln)
                c.act(r2, r2, AF.Exp, scale=-1.0)
                c.tt(o2, psO[:, :], r2, ALU.mult)
                c.release(psO)
                oo = on[:, 0:TB]
                sq = rr[:, 0:TB]
                sqb = qraw[h % 2]
                c.stt(oo, ytmp[:, 3 * TB:4 * TB], nlam, ytmp[:, 2 * TB:3 * TB], ALU.mult, ALU.add)
                c.act(sqb[:], oo, AF.Square)
                pss = c.psum()
                c.mm(pss[:, 0:TB], onesb[:], sqb[:])
                c.act(sq, pss[:, 0:TB], AF.Ln, scale=1.0 / 128, bias=1e-5)
                c.act(sq, sq, AF.Exp, scale=-0.5)
                c.stt(yaT[:, h, :], oo, subw, sq, ALU.mult, ALU.mult)

            sgen = ssd_gen()
            n_iter = 8 * (nkb + LAG)
            stride = max(1, n_iter // 26)
            it_cnt = [0]

            def ssd_step():
                it_cnt[0] += 1
                if it_cnt[0] % stride == 0:
                    next(sgen, None)

            pend = None
            for h in range(8):
                psO = c.psum(hold=True)
                Pacc = PaccV[h % 2]
                pts = {}
                for kb in range(nkb + LAG):
                    if kb < nkb:
                        pss0 = c.psum()
                        pss1 = c.psum()
                        c.mm(pss0[:, 0:TB], KT[0:64, h, kb * 128:(kb + 1) * 128], qT[0:64, h, :],
                             signal=False)
                        c.mm(pss1[:, 0:TB], KT[64:128, h, kb * 128:(kb + 1) * 128], qT[64:128, h, :])
                    if kb >= LAG:
                        k2 = kb - LAG
                        c.mm(psO[:, :], Vt[:, k2, h * 128:(h + 1) * 128], pts[k2][:],
                             start=(k2 == 0), stop=(k2 == nkb - 1))
                    if kb < nkb:
                        pt = ptile[kb % 4]
                        c.act(pt[:, 0:TB], pss0[:, 0:TB], AF.Exp, scale=0.125)
                        c.act(pt[:, TB:2 * TB], pss1[:, 0:TB], AF.Exp, scale=0.125)
                        dk = kb - (nkb - NT)
                        if dk >= 0:
                            c.tt(pt.v(pt.t[:, :].rearrange("p (a b) -> p a b", a=2)),
                                 pt.v(pt.t[:, :].rearrange("p (a b) -> p a b", a=2)),
                                 maskb.v(maskb.t[:, dk:dk + 1, :].to_broadcast([128, 2, TB])), ALU.mult)
                        if kb == 0:
                            c.copy(Pacc[:, :], pt[:])
                        else:
                            c.tt(Pacc[:, :], Pacc[:, :], pt[:], ALU.add)
                        pts[kb] = pt
                    ssd_step()
                    if kb == LAG and pend is not None:
                        attn_finalize(*pend)
                        pend = None
                pend = (h, psO, Pacc)
            attn_finalize(*pend)
            for _ in sgen:
                pass

            ck(6)
            ck(7)
            for oc in range(0, 8, 4):
                for (wA, wG, cG, yT, first) in (("so", "in", C_GS, ysT, True), ("ao", "in", C_GA, yaT, False)):
                    s_a = load_slab(wA, 0, 8, oc * 128, 512)
                    s_g = load_slab(wG, 0, 8, cG + oc * 128, 512)

                    def g0(j, stt_, s_a=s_a, s_g=s_g, yT=yT):
                        w = slice(j * 128, (j + 1) * 128)
                        psA = c.psum()
                        for kc in range(8):
                            c.mm(psA[:, 0:TB], s_a[:, kc, w], yT[:, kc, :], start=(kc == 0), stop=(kc == 7))
                        psC = c.psum()
                        for kc in range(8):
                            c.mm(psC[:, 0:TB], s_g[:, kc, w], hT[:, kc, HX:HX + TB], start=(kc == 0), stop=(kc == 7))
                        stt_["A"] = psA
                        stt_["C"] = psC

                    def g1(j, stt_, first=first, oc=oc):
                        sg0 = TF()[:, 0:TB]
                        c.act(sg0, stt_["C"][:, 0:TB], AF.Sigmoid)
                        if first:
                            c.tt(m1b[j][:], stt_["A"][:, 0:TB], sg0, ALU.mult)
                        else:
                            mg2 = TF()[:, 0:TB]
                            c.tt(mg2, stt_["A"][:, 0:TB], sg0, ALU.mult)
                            c.tt(mgT[:, oc + j, :], m1b[j][:], mg2, ALU.add, eng="pool")

                    pipeline(4, [g0, g1])

            ck(8)
            def tm_norm_residual(wname, lhs_fn, nk, Gw):
                psF = [[psacc() for i in range(NT)] for hf in range(2)]
                for hf in range(2):
                    for k0 in range(0, nk, 8):
                        n = min(8, nk - k0)
                        s = load_slab(wname, k0, n, hf * 512, 512)
                        for i in range(NT):
                            for kk in range(n):
                                c.mm(psF[hf][i][:, :], lhs_fn(k0 + kk, i), s[:, kk, :],
                                     start=(k0 + kk == 0), stop=(k0 + kk == nk - 1))
                for i in range(NT):
                    c.act(junk[:, 0:512], psF[0][i][:, :], AF.Square, accum=ssq[:, 12:13])
                    c.act(junk[:, 512:1024], psF[1][i][:, :], AF.Square, accum=ssq[:, 13:14])
                    c.tt(ssq[:, 14:15], ssq[:, 12:13], ssq[:, 13:14], ALU.add)
                    rstd_from_ss(ssq[:, 15:16], ssq[:, 14:15], D, 1e-6)
                    for hf in range(2):
                        hs = slice(hf * 512, (hf + 1) * 512)
                        c.stt(ytmp[:, hs], psF[hf][i][:, :], ssq[:, 15:16], Gw[:, hs], ALU.mult, ALU.mult)
                    c.tt(xres[:, i, :], xres[:, i, :], ytmp[:], ALU.add)

            tm_norm_residual("out", lambda k, i: mgT[:, k, i * 128:(i + 1) * 128], 8, G1W)

            ck(9)
            norm_to_T(xres, A2, 16, b, h2halo, 2)
            fslab = {}

            def f0(jj, stt_):
                if jj % 4 == 0:
                    n = min(4, NJ - jj)
                    fslab["g"] = load_slab("up", 0, 8, jj * 128, n * 128)
                    fslab["v"] = load_slab("up", 0, 8, DFF + jj * 128, n * 128)
                w = slice((jj % 4) * 128, (jj % 4 + 1) * 128)
                psg = c.psum()
                for kc in range(8):
                    c.mm(psg[:, 0:2 + TB], fslab["g"][:, kc, w], hT[:, kc, HX - 2:HX + TB], start=(kc == 0), stop=(kc == 7))
                psv = c.psum()
                for kc in range(8):
                    c.mm(psv[:, 0:2 + TB], fslab["v"][:, kc, w], hT[:, kc, HX - 2:HX + TB], start=(kc == 0), stop=(kc == 7))
                stt_["g"] = {"ps": psg}
                stt_["v"] = {"ps": psv}

            cg = conv_stages(3, None, "cwf", "cbf", lambda jj: jj, None)[0]
            cv = conv_stages(3, None, "cwf", "cbf", lambda jj: NJ + jj, None)[0]

            def f1(jj, stt_):
                cg(jj, stt_["g"])
                cv(jj, stt_["v"])

            def f2(jj, stt_):
                gl = TF()[:, 0:TB]
                c.act(gl, stt_["g"]["acc"], AF.Gelu_apprx_tanh)
                c.tt(mT[:, jj, :], gl, stt_["v"]["acc"], ALU.mult, eng="pool")

            pipeline(NJ, [f0, f1, f2])

            ck(10)
            tm_norm_residual("dn", lambda k, i: mT[:, k, i * 128:(i + 1) * 128], NJ, G2W)
            c.dma("pool", y_d[b, t0:t0 + TB, :].rearrange("(i p) f -> p i f", p=128), xres[:])

    c.finish("pool", xb)
    c.finish("sp", xb)
    return nc, c


_CACHE = {}


def _get_nc(nseq, nblk):
    key = (nseq, nblk)
    if key not in _CACHE:
        _CACHE[key] = build_nc(nseq, nblk)[0]
    return _CACHE[key]


def kernel(**inputs):
    inp = {k: np.asarray(v) for k, v in inputs.items()}
    x = inp["x"].astype(np.float32, copy=False)
    B = x.shape[0]
    nseq = B // NCORES
    nblk = x.shape[1] // TB
    P, R = _host_params(inp)
    C = _host_consts()
    shared = {
        "w_ada": np.ascontiguousarray(inp["w_ada"][0], dtype=np.float32),
        "w_in": np.ascontiguousarray(inp["w_in"][0], dtype=np.float32),
        "w_ssd_o": np.ascontiguousarray(inp["w_ssd_o"][0], dtype=np.float32),
        "w_attn_o": np.ascontiguousarray(inp["w_attn_o"][0], dtype=np.float32),
        "w_out": np.ascontiguousarray(inp["w_out"][0], dtype=np.float32),
        "w_up": np.ascontiguousarray(inp["w_up"][0], dtype=np.float32),
        "w_down": np.ascontiguousarray(inp["w_down"][0], dtype=np.float32),
        "params": P, "rows": R, "consts": C,
    }
    in_maps = []
    for ci in range(NCORES):
        cs = inp["c"][ci * nseq:(ci + 1) * nseq].astype(np.float32)
        cT = np.ascontiguousarray(cs.reshape(nseq, 8, 128).transpose(2, 1, 0))
        m = dict(shared)
        m["x"] = np.ascontiguousarray(x[ci * nseq:(ci + 1) * nseq])
        m["cT"] = cT
        in_maps.append(m)
    nc = _get_nc(nseq, nblk)
    res = run_bass_kernel_spmd(nc, in_maps, core_ids=list(range(NCORES)))
    out = np.concatenate([np.asarray(r["y"]) for r in res.results], axis=0)
    return out.astype(np.float32, copy=False)
```
